# Optimizing a Trainium2 kernel written in Bass

```python
import math
import jax, jax.numpy as jnp
from jax import lax
import numpy as np

D_MODEL = 1024
BATCH = 32
SEQ = 2048
DEPTH = 2

CHUNK = 64
Q_BLOCK = 128
HEAD_DIM = 64
N_HEADS = D_MODEL // HEAD_DIM
N_GROUPS = 4
HEADS_PER_GROUP = N_HEADS // N_GROUPS
GROUP_WIDTH = HEADS_PER_GROUP * HEAD_DIM
MIX_WIDTH = N_GROUPS * GROUP_WIDTH
BAND_CHUNKS = 8
BAND = (BAND_CHUNKS + 1) * CHUNK
MAX_REL_DIST = 256
N_REL = 2 * MAX_REL_DIST + 1
DIFF_QK_DIM = HEAD_DIM // 2
D_FF = ((8 * D_MODEL // 3 + 255) // 256) * 256
FFN_RES = 0.5
RMS_EPS = 1e-6
GROUP_SPLIT_SIZES = (GROUP_WIDTH, GROUP_WIDTH, GROUP_WIDTH, HEADS_PER_GROUP,
                     GROUP_WIDTH, GROUP_WIDTH, GROUP_WIDTH,
                     GROUP_WIDTH, GROUP_WIDTH, GROUP_WIDTH,
                     GROUP_WIDTH, GROUP_WIDTH, GROUP_WIDTH)
IN_WIDTH = 12 * GROUP_WIDTH + HEADS_PER_GROUP

kernel_name = 'hybrid_chunk_causal_parallel_heads'


def _split_points():
    return [int(p) for p in np.cumsum(GROUP_SPLIT_SIZES)[:-1]]


def rmsnorm(x, g):
    xf = x.astype(jnp.float32)
    y = xf * lax.rsqrt(jnp.mean(xf * xf, axis=-1, keepdims=True) + RMS_EPS)
    return (y * g.astype(jnp.float32)).astype(x.dtype)


def swiglu(h, w_gu, w_down):
    gate, up = jnp.split(h @ w_gu, 2, axis=-1)
    return (jax.nn.silu(gate) * up) @ w_down


def to_heads(t):
    b, s, w = t.shape
    return t.reshape(b, s, w // HEAD_DIM, HEAD_DIM).transpose(0, 2, 1, 3)


def from_heads(t):
    b, h, s, d = t.shape
    return t.transpose(0, 2, 1, 3).reshape(b, s, h * d)


def alibi_slopes(n):
    return jnp.asarray([2.0 ** (-8.0 * (i + 1) / n) for i in range(n)], dtype=jnp.float32)


def lambda_init(layer_idx):
    return 0.8 - 0.6 * math.exp(-0.3 * layer_idx)


def forgetting_attention(q, k, v, log_f):
    seq, d = q.shape[2], q.shape[3]
    scale = d ** -0.5
    cum_f = jnp.cumsum(log_f, axis=-1)
    outs = []
    for i in range(seq // Q_BLOCK):
        q0, q1 = i * Q_BLOCK, (i + 1) * Q_BLOCK
        logits = jnp.einsum('bhqd,bhkd->bhqk', q[:, :, q0:q1], k[:, :, :q1]).astype(jnp.float32) * scale
        logits = logits + cum_f[:, :, q0:q1, None] - cum_f[:, :, None, :q1]
        t = jnp.arange(q0, q1)[:, None]
        s = jnp.arange(q1)[None, :]
        logits = jnp.where(s <= t, logits, -jnp.inf)
        p = jax.nn.softmax(logits, axis=-1)
        outs.append(jnp.einsum('bhqk,bhkd->bhqd', p.astype(v.dtype), v[:, :, :q1]))
    return jnp.concatenate(outs, axis=2)


def chunked_relpos_attention(q, k, v, rel_table):
    b, h, seq, d = q.shape
    scale = d ** -0.5
    n_chunks = seq // CHUNK
    pad = BAND_CHUNKS * CHUNK
    kp = jnp.pad(k, ((0, 0), (0, 0), (pad, 0), (0, 0)))
    vp = jnp.pad(v, ((0, 0), (0, 0), (pad, 0), (0, 0)))
    rel = pad + np.arange(CHUNK)[:, None] - np.arange(BAND)[None, :]
    rel_idx = np.clip(rel, -MAX_REL_DIST, MAX_REL_DIST) + MAX_REL_DIST
    bias = rel_table.astype(jnp.float32)[:, rel_idx]

    def chunk_fn(c):
        qc = lax.dynamic_slice_in_dim(q, c * CHUNK, CHUNK, axis=2)
        kc = lax.dynamic_slice_in_dim(kp, c * CHUNK, BAND, axis=2)
        vc = lax.dynamic_slice_in_dim(vp, c * CHUNK, BAND, axis=2)
        logits = jnp.einsum('bhqd,bhkd->bhqk', qc, kc).astype(jnp.float32) * scale + bias
        key_pos = c * CHUNK - pad + jnp.arange(BAND)
        logits = jnp.where(key_pos >= 0, logits, -jnp.inf)
        p = jax.nn.softmax(logits, axis=-1)
        return jnp.einsum('bhqk,bhkd->bhqd', p.astype(v.dtype), vc)

    out = lax.map(chunk_fn, jnp.arange(n_chunks))
    return out.transpose(1, 2, 0, 3, 4).reshape(b, h, seq, d)


def diff_attention(q, k, v, lam_params, lam_init):
    seq = q.shape[2]
    q1, q2 = q[..., :DIFF_QK_DIM], q[..., DIFF_QK_DIM:]
    k1, k2 = k[..., :DIFF_QK_DIM], k[..., DIFF_QK_DIM:]
    lp = lam_params.astype(jnp.float32)
    lam = jnp.exp(jnp.sum(lp[0] * lp[1])) - jnp.exp(jnp.sum(lp[2] * lp[3])) + lam_init
    scale = DIFF_QK_DIM ** -0.5
    slopes = alibi_slopes(q.shape[1])[:, None, None]
    outs = []
    for i in range(seq // Q_BLOCK):
        q0, q1e = i * Q_BLOCK, (i + 1) * Q_BLOCK
        t = jnp.arange(q0, q1e)[:, None]
        s = jnp.arange(q1e)[None, :]
        alibi = -slopes * jnp.abs(t - s).astype(jnp.float32)
        mask = (s // CHUNK) <= (t // CHUNK)
        l1 = jnp.einsum('bhqd,bhkd->bhqk', q1[:, :, q0:q1e], k1[:, :, :q1e]).astype(jnp.float32) * scale + alibi
        l2 = jnp.einsum('bhqd,bhkd->bhqk', q2[:, :, q0:q1e], k2[:, :, :q1e]).astype(jnp.float32) * scale + alibi
        p = (jax.nn.softmax(jnp.where(mask, l1, -jnp.inf), axis=-1)
             - lam * jax.nn.softmax(jnp.where(mask, l2, -jnp.inf), axis=-1))
        outs.append(jnp.einsum('bhqk,bhkd->bhqd', p.astype(v.dtype), v[:, :, :q1e]))
    o = jnp.concatenate(outs, axis=2).astype(jnp.float32)
    o = o * lax.rsqrt(jnp.mean(o * o, axis=-1, keepdims=True) + RMS_EPS) * (1.0 - lam_init)
    return o.astype(v.dtype)


def stick_breaking_attention(q, k, v):
    seq, d = q.shape[2], q.shape[3]
    scale = d ** -0.5
    outs = []
    for i in range(seq // Q_BLOCK):
        q0, q1 = i * Q_BLOCK, (i + 1) * Q_BLOCK
        z = jnp.einsum('bhqd,bhkd->bhqk', q[:, :, q0:q1], k[:, :, :q1]).astype(jnp.float32) * scale
        t = jnp.arange(q0, q1)[:, None]
        s = jnp.arange(q1)[None, :]
        strict = s < t
        log_one_minus = jnp.where(strict, jax.nn.log_sigmoid(-z), 0.0)
        suffix = lax.cumsum(log_one_minus, axis=3, reverse=True) - log_one_minus
        weights = jnp.where(strict, jnp.exp(jax.nn.log_sigmoid(z) + suffix), 0.0)
        outs.append(jnp.einsum('bhqk,bhkd->bhqd', weights.astype(v.dtype), v[:, :, :q1]))
    return jnp.concatenate(outs, axis=2)


def hybrid_mixer(h, w_in, b_f, rel_table, lam_params, w_out, lam_init):
    proj = h @ w_in
    (qa, ka, va, fa, qb, kb, vb, qc, kc, vc, qd, kd, vd) = jnp.split(proj, _split_points(), axis=-1)
    log_f = jax.nn.log_sigmoid((fa + b_f).astype(jnp.float32)).transpose(0, 2, 1)
    o_a = forgetting_attention(to_heads(qa), to_heads(ka), to_heads(va), log_f)
    o_b = chunked_relpos_attention(to_heads(qb), to_heads(kb), to_heads(vb), rel_table)
    o_c = diff_attention(to_heads(qc), to_heads(kc), to_heads(vc), lam_params, lam_init)
    o_d = stick_breaking_attention(to_heads(qd), to_heads(kd), to_heads(vd))
    mixed = jnp.concatenate([from_heads(o_a), from_heads(o_b), from_heads(o_c), from_heads(o_d)], axis=-1)
    return mixed @ w_out


def setup_inputs(seed: int = 0) -> dict:
    key = jax.random.key(seed)
    ks = jax.random.split(key, 16)
    f32 = jnp.float32
    x = jax.random.normal(ks[0], (BATCH, SEQ, D_MODEL), f32)
    g_ffn1 = 1.0 + 0.02 * jax.random.normal(ks[1], (DEPTH, D_MODEL), f32)
    ffn1_w_gu = jax.random.normal(ks[2], (DEPTH, D_MODEL, 2 * D_FF), f32) * D_MODEL ** -0.5
    ffn1_w_down = jax.random.normal(ks[3], (DEPTH, D_FF, D_MODEL), f32) * D_FF ** -0.5
    g_mix = 1.0 + 0.02 * jax.random.normal(ks[4], (DEPTH, D_MODEL), f32)
    w_in = jax.random.normal(ks[5], (DEPTH, D_MODEL, IN_WIDTH), f32) * D_MODEL ** -0.5
    b_f = 0.1 * jax.random.normal(ks[6], (DEPTH, HEADS_PER_GROUP), f32)
    rel_bias = 0.2 * jax.random.normal(ks[7], (DEPTH, HEADS_PER_GROUP, N_REL), f32)
    diff_lambda = 0.1 * jax.random.normal(ks[8], (DEPTH, 4, DIFF_QK_DIM), f32)
    w_out = jax.random.normal(ks[9], (DEPTH, MIX_WIDTH, D_MODEL), f32) * MIX_WIDTH ** -0.5
    g_ffn2 = 1.0 + 0.02 * jax.random.normal(ks[10], (DEPTH, D_MODEL), f32)
    ffn2_w_gu = jax.random.normal(ks[11], (DEPTH, D_MODEL, 2 * D_FF), f32) * D_MODEL ** -0.5
    ffn2_w_down = jax.random.normal(ks[12], (DEPTH, D_FF, D_MODEL), f32) * D_FF ** -0.5
    g_final = 1.0 + 0.02 * jax.random.normal(ks[13], (D_MODEL,), f32)
    return {'x': x, 'g_ffn1': g_ffn1, 'ffn1_w_gu': ffn1_w_gu, 'ffn1_w_down': ffn1_w_down,
            'g_mix': g_mix, 'w_in': w_in, 'b_f': b_f, 'rel_bias': rel_bias,
            'diff_lambda': diff_lambda, 'w_out': w_out, 'g_ffn2': g_ffn2,
            'ffn2_w_gu': ffn2_w_gu, 'ffn2_w_down': ffn2_w_down, 'g_final': g_final}


def reference(x, g_ffn1, ffn1_w_gu, ffn1_w_down, g_mix, w_in, b_f, rel_bias, diff_lambda,
              w_out, g_ffn2, ffn2_w_gu, ffn2_w_down, g_final):
    for l in range(DEPTH):
        x = x + FFN_RES * swiglu(rmsnorm(x, g_ffn1[l]), ffn1_w_gu[l], ffn1_w_down[l])
        x = x + hybrid_mixer(rmsnorm(x, g_mix[l]), w_in[l], b_f[l], rel_bias[l], diff_lambda[l],
                             w_out[l], lambda_init(l))
        x = x + FFN_RES * swiglu(rmsnorm(x, g_ffn2[l]), ffn2_w_gu[l], ffn2_w_down[l])
    return rmsnorm(x, g_final)
```

```python
import math
from contextlib import ExitStack
import numpy as np
import ml_dtypes
import concourse.bass as bass
import concourse.mybir as mybir
from concourse.bass_utils import run_bass_kernel_spmd

F32 = mybir.dt.float32
BF16 = mybir.dt.bfloat16
AF = mybir.ActivationFunctionType
ALU = mybir.AluOpType

S = 2048
D = 1024
DFF = 2816
NJ = 22
INW = 3076
NL = 2
EPS = 1e-6
EPOCH = 30000
NCORES = 8
NSEQ = 4
MIXERS = (0, 1, 2, 3)


class _Eng:
    def __init__(self, fw, name, handle, nsem):
        self.name = name
        self.h = handle
        self.sems = [fw.nc.alloc_semaphore(f"s_{name}_{i}") for i in range(nsem)]
        self.count = 0
        self.known = {}


class FW:
    def __init__(self, nc, nsem=4):
        self.nc = nc
        self.pe = _Eng(self, "pe", nc.tensor, nsem)
        self.act = _Eng(self, "act", nc.scalar, nsem)
        self.dve = _Eng(self, "dve", nc.vector, nsem)
        self.pool = _Eng(self, "pool", nc.gpsimd, nsem)
        self.sp = _Eng(self, "sp", nc.sync, 1)
        self.engs = {e.name: e for e in (self.pe, self.act, self.dve, self.pool, self.sp)}
        self.lastw = {}
        self.readers = {}
        self.dma_sems = {}
        self.n_wait = 0
        self.n_inst = 0

    def _wait(self, eng, ev):
        if ev is None:
            return
        if ev[0] == 'e':
            src = self.engs[ev[1]]
            idx = ev[2]
            if src is eng and eng is self.pe:
                return
            if eng.known.get(src.name, 0) >= idx + 1:
                return
            eng.h.wait_ge(src.sems[idx // EPOCH], idx % EPOCH + 1)
            self.n_wait += 1
            eng.known[src.name] = idx + 1
        else:
            st, val = ev[1], ev[2]
            key = ('d', st)
            if eng.known.get(key, 0) >= val:
                return
            eng.h.wait_ge(self.dma_sems[st][0], val)
            self.n_wait += 1
            eng.known[key] = val

    def _deps(self, eng, reads, writes):
        for k in reads:
            self._wait(eng, self.lastw.get(k))
        for k in writes:
            self._wait(eng, self.lastw.get(k))
            for ev in self.readers.get(k, ()):
                self._wait(eng, ev)

    def _commit(self, ev, reads, writes):
        for k in reads:
            if not k.startswith("c_"):
                self.readers.setdefault(k, []).append(ev)
        for k in writes:
            self.lastw[k] = ev
            self.readers[k] = []

    def op(self, eng, fns, reads=(), writes=()):
        self._deps(eng, reads, writes)
        if not isinstance(fns, (list, tuple)):
            fns = [fns]
        inst = None
        for fn in fns:
            inst = fn(eng.h)
        idx = eng.count
        inst.then_inc(eng.sems[idx // EPOCH], 1)
        eng.count += 1
        self.n_inst += len(fns)
        self._commit(('e', eng.name, idx), reads, writes)

    def dma(self, queue, stream, out, in_, reads=(), writes=(), **kw):
        if stream not in self.dma_sems:
            self.dma_sems[stream] = [self.nc.alloc_semaphore(f"d_{stream}"), 0]
        self._deps(queue, reads, writes)
        ds = self.dma_sems[stream]
        queue.h.dma_start(out=out, in_=in_, **kw).then_inc(ds[0], 16)
        ds[1] += 16
        self.n_inst += 1
        self._commit(('d', stream, ds[1]), reads, writes)

    def barrier(self):
        evs = []
        for e in (self.pe, self.act, self.dve, self.pool):
            if e.count:
                evs.append(('e', e.name, e.count - 1))
        for st, (sem, val) in self.dma_sems.items():
            if val:
                evs.append(('d', st, val))
        for e in (self.pe, self.act, self.dve, self.pool, self.sp):
            for ev in evs:
                self._wait(e, ev)
        self.lastw = {}
        self.readers = {}


def _consts():
    c = {}
    c["ident"] = np.eye(128, dtype=np.float32)
    kk = np.arange(128)[:, None]
    qq = np.arange(128)[None, :]
    bf = ml_dtypes.bfloat16
    m = np.zeros((128, 5, 128), np.float32)
    m[:, 0, :] = (qq >= kk)
    m[:, 1, :] = (qq > kk)
    m[:, 2, :] = (kk > qq)
    m[:, 3, :] = (kk <= qq)
    m[:, 4, :] = 1.0
    c["masks"] = m.astype(bf)
    slopes = [2.0 ** (-8.0 * (i + 1) / 4) for i in range(4)]
    mc = np.zeros((128, 5, 128), np.float32)
    vis = ((kk // 64) <= (qq // 64))
    for h in range(4):
        corr = np.where(kk > qq, np.exp(-2.0 * slopes[h] * (kk - qq).astype(np.float64)), 1.0)
        mc[:, h, :] = np.where(vis, corr, 0.0)
    mc[:, 4, :] = np.where(qq >= kk, 0.0, -30000.0)
    c["maskc"] = mc
    t = np.arange(S)
    caug = np.zeros((4, 2, 4, S), np.float32)
    for h in range(4):
        sl = slopes[h]
        caug[h, 0, 0] = -sl * 64 * (t // 64)
        caug[h, 0, 1] = -sl * (t % 64)
        caug[h, 0, 2] = 1.0
        caug[h, 0, 3] = 1.0
        caug[h, 1, 0] = 1.0
        caug[h, 1, 1] = 1.0
        caug[h, 1, 2] = sl * 64 * (t // 64)
        caug[h, 1, 3] = sl * (t % 64)
    c["caug"] = caug.astype(bf)
    kk6 = np.arange(128)[:, None]
    qq6 = np.arange(640)[None, :]
    visb = ((kk6 // 64) <= (qq6 // 64)) & ((qq6 // 64) <= (kk6 // 64) + 8)
    c["visb"] = visb.astype(np.float32).astype(bf)
    cc = np.zeros((128, 16), np.float32)
    cc[:, 0] = EPS
    cc[:, 1] = 1.0
    for p in range(24):
        i = p % 6
        if i == 0: cc[p, 2] = -1.0
        if i == 1: cc[p, 3] = -1.0
        if i == 2: cc[p, 4] = -1.0
        if i >= 3: cc[p, 5] = 1.0
        if i == 3: cc[p, 6] = 1.0
        if i == 4: cc[p, 7] = 1.0
        if i == 5: cc[p, 8] = 1.0
        if i < 3: cc[p, 9] = 1.0
    c["cc"] = cc
    return c


def _lam_init(l):
    return 0.8 - 0.6 * math.exp(-0.3 * l)


def build(nseq=NSEQ, stages=None, final_norm=True):
    nc = bass.Bass("TRN2", target_bir_lowering=False)
    fw = FW(nc)
    pe, act, dve, pool, sp = fw.pe, fw.act, fw.dve, fw.pool, fw.sp

    uid = [0]

    def SB(name, shape, dt):
        uid[0] += 1
        return nc.sbuf_tensor(f"{name}_{uid[0]}", shape, dt)

    def din(name, shape, dt=F32):
        return nc.dram_tensor(name, list(shape), dt, kind="ExternalInput")

    x_d = din("x", [nseq, S, D])
    wgu_d = [din("ffn1_w_gu", [NL, D, 2 * DFF]), din("ffn2_w_gu", [NL, D, 2 * DFF])]
    wd_d = [din("ffn1_w_down", [NL, DFF, D]), din("ffn2_w_down", [NL, DFF, D])]
    win_d = din("w_in", [NL, D, INW])
    wout_d = din("w_out", [NL, D, D])
    gall_d = din("gall", [128, 56])
    bfb_d = din("bfb", [128, 2])
    dlb_d = din("dlb", [128, 256])
    relb_d = din("relb", [NL, 128, 4, 640])
    ident_d = din("ident", [128, 128])
    masks_d = din("masks", [128, 5, 128], BF16)
    maskc_d = din("maskc", [128, 5, 128])
    caug_d = din("caug", [4, 2, 4, S], BF16)
    visb_d = din("visb", [128, 640], BF16)
    cc_d = din("cc", [128, 16])
    y_d = nc.dram_tensor("y", [nseq, S, D], F32, kind="ExternalOutput")

    wgu_s = nc.dram_tensor("wgu_s", [NL, 2, NJ, 128, 2048], BF16, kind="Internal")
    wd_s = nc.dram_tensor("wd_s", [NL, 2, 8, 128, NJ * 128], BF16, kind="Internal")
    win_s = nc.dram_tensor("win_s", [NL, D, INW], BF16, kind="Internal")
    wout_s = nc.dram_tensor("wout_s", [NL, D, D], BF16, kind="Internal")

    pb = [nc.alloc_psum_tensor(f"pb{i}", [128, 512], F32) for i in range(8)]

    ident = nc.alloc_sbuf_tensor("sb_ident", [128, 128], F32)
    masks = nc.alloc_sbuf_tensor("sb_masks", [128, 5, 128], BF16)
    maskc = nc.alloc_sbuf_tensor("sb_maskc", [128, 5, 128], F32)
    cc = nc.alloc_sbuf_tensor("sb_cc", [128, 16], F32)
    gall = nc.alloc_sbuf_tensor("sb_gall", [128, 56], F32)
    bfb = nc.alloc_sbuf_tensor("sb_bfb", [128, 2], F32)
    lamc = nc.alloc_sbuf_tensor("lamc", [128, 8], F32)
    ones_b = masks[:, 4, :]
    mA = masks[:, 0, :]
    mD = masks[:, 1, :]
    mU = masks[:, 2, :]
    mLI = masks[:, 3, :]

    for (t, d, k) in ((ident, ident_d, "c_ident"), (masks, masks_d, "c_masks"), (maskc, maskc_d, "c_maskc"),
                      (cc, cc_d, "c_cc"), (gall, gall_d, "c_gall"), (bfb, bfb_d, "c_bfb")):
        fw.dma(sp, "cst_" + k, t[:], d.ap(), writes=[k])

    with ExitStack() as _es:
        dl = _es.enter_context(SB("dl", [128, 256], F32))
        dlt = _es.enter_context(SB("dlt", [128, 64], F32))
        dls = _es.enter_context(SB("dls", [128, 8], F32))
        fw.dma(sp, "cst_dl", dl[:], dlb_d.ap(), writes=["dl"])
        for l in range(NL):
            for pr in range(2):
                a0 = l * 128 + pr * 64
                fw.op(dve, lambda e, a0=a0, pr=pr: e.tensor_tensor(out=dlt[:, pr * 32:pr * 32 + 32], in0=dl[:, a0:a0 + 32],
                                                                  in1=dl[:, a0 + 32:a0 + 64], op=ALU.mult),
                      reads=["dl"], writes=["dlt"])
                fw.op(dve, lambda e, l=l, pr=pr: e.reduce_sum(out=dls[:, l * 2 + pr:l * 2 + pr + 1],
                                                              in_=dlt[:, pr * 32:pr * 32 + 32], axis=mybir.AxisListType.X),
                      reads=["dlt"], writes=["dls"])
        fw.op(act, lambda e: e.activation(out=dls[:, 4:8], in_=dls[:, 0:4], func=AF.Exp), reads=["dls"], writes=["dls"])
        for l in range(NL):
            fw.op(dve, lambda e, l=l: e.tensor_tensor(out=lamc[:, l:l + 1], in0=dls[:, 4 + 2 * l + 1:4 + 2 * l + 2],
                                                      in1=dls[:, 4 + 2 * l:4 + 2 * l + 1], op=ALU.subtract),
                  reads=["dls"], writes=["c_lamc"])
            fw.op(dve, lambda e, l=l: e.tensor_scalar(out=lamc[:, l:l + 1], in0=lamc[:, l:l + 1], scalar1=-_lam_init(l),
                                                      scalar2=None, op0=ALU.add),
                  reads=["c_lamc"], writes=["c_lamc"])
        fw.barrier()

    def prepass():
        with ExitStack() as _es:
            f0 = _es.enter_context(SB("ppf0", [128, 5632], F32))
            f1 = _es.enter_context(SB("ppf1", [128, 5632], F32))
            b0 = _es.enter_context(SB("ppb0", [128, 5632], BF16))
            b1 = _es.enter_context(SB("ppb1", [128, 5632], BF16))
            fbs, bbs = (f0, f1), (b0, b1)
            items = []
            for l in range(NL):
                for f in range(2):
                    for kc in range(8):
                        def st(bt, i, l=l, f=f, kc=kc):
                            for half in range(2):
                                dst = bass.AP(wgu_s, ((l * 2 + f) * NJ) * 128 * 2048 + kc * 256 + half * 128,
                                              [[2048, 128], [128 * 2048, NJ], [1, 128]])
                                src = bt[:, half * DFF:(half + 1) * DFF].rearrange("p (j c) -> p j c", c=128)
                                fw.dma(sp, f"pps{i % 2}", dst, src, reads=[f"ppb{i % 2}"], writes=["scr"])
                        items.append((wgu_d[f][l, kc * 128:(kc + 1) * 128, :], 2 * DFF, st))
                    for j in range(NJ):
                        def st(bt, i, l=l, f=f, j=j):
                            dst = bass.AP(wd_s, ((l * 2 + f) * 8) * 128 * NJ * 128 + j * 128,
                                          [[NJ * 128, 128], [128 * NJ * 128, 8], [1, 128]])
                            src = bt[:, 0:D].rearrange("p (o c) -> p o c", c=128)
                            fw.dma(sp, f"pps{i % 2}", dst, src, reads=[f"ppb{i % 2}"], writes=["scr"])
                        items.append((wd_d[f][l, j * 128:(j + 1) * 128, :], D, st))
                for kc in range(8):
                    def st(bt, i, l=l, kc=kc):
                        fw.dma(sp, f"pps{i % 2}", win_s[l, kc * 128:(kc + 1) * 128, :], bt[:, 0:INW], reads=[f"ppb{i % 2}"], writes=["scr"])
                    items.append((win_d[l, kc * 128:(kc + 1) * 128, :], INW, st))

                    def st2(bt, i, l=l, kc=kc):
                        fw.dma(sp, f"pps{i % 2}", wout_s[l, kc * 128:(kc + 1) * 128, :], bt[:, 0:D], reads=[f"ppb{i % 2}"], writes=["scr"])
                    items.append((wout_d[l, kc * 128:(kc + 1) * 128, :], D, st2))

            def load(i):
                src, C, _ = items[i]
                fw.dma(sp, f"ppl{i % 2}", fbs[i % 2][:, 0:C], src, writes=[f"ppf{i % 2}"])
            load(0)
            for i, (src, C, st) in enumerate(items):
                if i + 1 < len(items):
                    load(i + 1)
                fb, bb = fbs[i % 2], bbs[i % 2]
                k = i % 3
                if k == 0:
                    fw.op(pool, lambda e, fb=fb, bb=bb, C=C: e.tensor_copy(out=bb[:, 0:C], in_=fb[:, 0:C]),
                          reads=[f"ppf{i % 2}"], writes=[f"ppb{i % 2}"])
                elif k == 1:
                    fw.op(dve, lambda e, fb=fb, bb=bb, C=C: e.tensor_copy(out=bb[:, 0:C], in_=fb[:, 0:C]),
                          reads=[f"ppf{i % 2}"], writes=[f"ppb{i % 2}"])
                else:
                    fw.op(act, lambda e, fb=fb, bb=bb, C=C: e.activation(out=bb[:, 0:C], in_=fb[:, 0:C], func=AF.Copy),
                          reads=[f"ppf{i % 2}"], writes=[f"ppb{i % 2}"])
                st(bb, i)
            fw.barrier()

    prepass()

    xT = nc.alloc_sbuf_tensor("xT", [128, 8, S], F32)
    hT = nc.alloc_sbuf_tensor("hT", [128, 8, S], BF16)
    sq = nc.alloc_sbuf_tensor("sq", [128, 8, 512], BF16)
    lnv = nc.alloc_sbuf_tensor("lnv", [128, 512], F32)
    rstd = nc.alloc_sbuf_tensor("rstd", [128, 512], F32)

    def xk(fc, tt):
        return f"x{fc}_{tt}"

    def tsl(tt):
        return slice(tt * 512, (tt + 1) * 512)

    def norm_stats(tt, bank):
        fw.op(act, lambda e: e.activation(out=sq[:], in_=xT[:, :, tsl(tt)], func=AF.Square),
              reads=[xk(fc, tt) for fc in range(8)], writes=["sq"])
        fw.op(pe, [lambda e, fc=fc: e.matmul(pb[bank][:], lhsT=ones_b, rhs=sq[:, fc, :], start=(fc == 0), stop=(fc == 7))
                   for fc in range(8)], reads=["sq", "c_masks"], writes=[f"pb{bank}"])
        fw.op(act, lambda e: e.activation(out=lnv[:], in_=pb[bank][:], func=AF.Ln, bias=cc[:, 0:1], scale=1.0 / D),
              reads=[f"pb{bank}", "c_cc"], writes=["lnv"])
        fw.op(act, lambda e: e.activation(out=rstd[:], in_=lnv[:], func=AF.Exp, scale=-0.5),
              reads=["lnv"], writes=["rstd"])

    def norm_to_h(gidx, tt, hcol0, bank=6):
        norm_stats(tt, bank)
        for fc in range(8):
            fw.op(dve, lambda e, fc=fc: e.scalar_tensor_tensor(
                out=hT[:, fc, hcol0:hcol0 + 512], in0=xT[:, fc, tsl(tt)], scalar=gall[:, gidx * 8 + fc:gidx * 8 + fc + 1],
                in1=rstd[:], op0=ALU.mult, op1=ALU.mult),
                reads=[xk(fc, tt), "rstd", "c_gall"], writes=[f"h{hcol0 // 512}"])

    def xfer(b_store, b_load, do_norm):
        with ExitStack() as _es:
            if b_store is not None:
                yT = _es.enter_context(SB("yT", [128, 8, 512], F32))
                yts = (_es.enter_context(SB("ytok0", [128, D], F32)), _es.enter_context(SB("ytok1", [128, D], F32)))
            if b_load is not None:
                xts = (_es.enter_context(SB("xtok0", [128, D], F32)), _es.enter_context(SB("xtok1", [128, D], F32)))
            cnt = 0
            for tt in range(4):
                if b_store is not None:
                    if do_norm:
                        norm_stats(tt, 5)
                        for fc in range(8):
                            fw.op(dve, lambda e, fc=fc: e.scalar_tensor_tensor(
                                out=yT[:, fc, :], in0=xT[:, fc, tsl(tt)], scalar=gall[:, 48 + fc:48 + fc + 1],
                                in1=rstd[:], op0=ALU.mult, op1=ALU.mult),
                                reads=[xk(fc, tt), "rstd", "c_gall"], writes=["yT"])
                    else:
                        fw.op(dve, lambda e: e.tensor_copy(out=yT[:], in_=xT[:, :, tsl(tt)]),
                              reads=[xk(fc, tt) for fc in range(8)], writes=["yT"])
                    for tcl in range(4):
                        yt = yts[cnt % 2]
                        for g in range(2):
                            bank = 6 + g
                            fw.op(pe, [lambda e, q=q, g=g, bank=bank, tcl=tcl: e.transpose(
                                pb[bank][:, q * 128:(q + 1) * 128], yT[:, g * 4 + q, tcl * 128:(tcl + 1) * 128], ident[:])
                                for q in range(4)], reads=["yT", "c_ident"], writes=[f"pb{bank}"])
                            if g == 0:
                                fw.op(act, lambda e, yt=yt, bank=bank: e.activation(out=yt[:, 0:512], in_=pb[bank][:], func=AF.Copy),
                                      reads=[f"pb{bank}"], writes=[f"ytok{cnt % 2}a"])
                            else:
                                fw.op(dve, lambda e, yt=yt, bank=bank: e.tensor_copy(out=yt[:, 512:1024], in_=pb[bank][:]),
                                      reads=[f"pb{bank}"], writes=[f"ytok{cnt % 2}b"])
                        r0 = tt * 512 + tcl * 128
                        fw.dma(sp, f"yst{cnt % 2}", y_d[b_store, r0:r0 + 128, :], yt[:],
                               reads=[f"ytok{cnt % 2}a", f"ytok{cnt % 2}b"], writes=["y_out"])
                        cnt += 1
                if b_load is not None:
                    for tc in range(tt * 4, tt * 4 + 4):
                        xt = xts[tc % 2]
                        fw.dma(sp, f"xl{tc % 2}", xt[:], x_d[b_load, tc * 128:(tc + 1) * 128, :], writes=[f"xtok{tc % 2}"])
                        for g in range(2):
                            bank = 3 + g
                            fw.op(pe, [lambda e, q=q, g=g, xt=xt, bank=bank: e.transpose(
                                pb[bank][:, q * 128:(q + 1) * 128], xt[:, (g * 4 + q) * 128:(g * 4 + q + 1) * 128], ident[:])
                                for q in range(4)], reads=[f"xtok{tc % 2}", "c_ident"], writes=[f"pb{bank}"])
                            src = pb[bank][:].rearrange("p (q c) -> p q c", c=128)
                            dst = xT[:, g * 4:(g + 1) * 4, tc * 128:(tc + 1) * 128]
                            wk = [xk(fc, tt) for fc in range(g * 4, g * 4 + 4)]
                            if g == 0:
                                fw.op(act, lambda e, src=src, dst=dst: e.activation(out=dst, in_=src, func=AF.Copy),
                                      reads=[f"pb{bank}"], writes=wk)
                            else:
                                fw.op(dve, lambda e, src=src, dst=dst: e.tensor_copy(out=dst, in_=src),
                                      reads=[f"pb{bank}"], writes=wk)
            fw.barrier()

    def ffn(l, f):
        gidx = l * 3 + (0 if f == 0 else 2)
        with ExitStack() as _es:
            actT = _es.enter_context(SB("actT", [128, NJ, 1024], BF16))
            wg0 = _es.enter_context(SB("wgb0", [128, 8, 256], BF16))
            wg1 = _es.enter_context(SB("wgb1", [128, 8, 256], BF16))
            wg2 = _es.enter_context(SB("wgb2", [128, 8, 256], BF16))
            wdb0 = _es.enter_context(SB("wdb0", [128, NJ, 128], BF16))
            wdb1 = _es.enter_context(SB("wdb1", [128, NJ, 128], BF16))
            sg0 = _es.enter_context(SB("sg0", [128, 512], F32))
            sg1 = _es.enter_context(SB("sg1", [128, 512], F32))
            wgs = (wg0, wg1, wg2)
            wds = (wdb0, wdb1)
            sgs = (sg0, sg1)
            st = {"g": 0, "d": 0, "wg": 0, "wd": 0}

            def do_norm(half):
                for t2 in range(2):
                    norm_to_h(gidx, half * 2 + t2, half * 1024 + t2 * 512)

            def do_gu(half):
                for j in range(NJ):
                    wi = st["wg"] % 3
                    st["wg"] += 1
                    wg = wgs[wi]
                    fw.dma(sp, f"wg{wi}", wg[:].rearrange("p a b -> p (a b)"), wgu_s[l, f, j], reads=["scr"], writes=[f"wg{wi}"])
                    for t2 in range(2):
                        c = st["g"] % 2
                        st["g"] += 1
                        gb, ub = 2 * c, 2 * c + 1
                        hc = half * 1024 + t2 * 512
                        hk = f"h{hc // 512}"
                        fw.op(pe, [lambda e, kc=kc, wg=wg, gb=gb, hc=hc: e.matmul(
                            pb[gb][:], lhsT=wg[:, kc, 0:128], rhs=hT[:, kc, hc:hc + 512], start=(kc == 0), stop=(kc == 7))
                            for kc in range(8)], reads=[f"wg{wi}", hk], writes=[f"pb{gb}"])
                        fw.op(pe, [lambda e, kc=kc, wg=wg, ub=ub, hc=hc: e.matmul(
                            pb[ub][:], lhsT=wg[:, kc, 128:256], rhs=hT[:, kc, hc:hc + 512], start=(kc == 0), stop=(kc == 7))
                            for kc in range(8)], reads=[f"wg{wi}", hk], writes=[f"pb{ub}"])
                        sg = sgs[c]
                        fw.op(act, lambda e, sg=sg, gb=gb: e.activation(out=sg[:], in_=pb[gb][:], func=AF.Silu),
                              reads=[f"pb{gb}"], writes=[f"sg{c}"])
                        fw.op(dve, lambda e, sg=sg, ub=ub, j=j, t2=t2: e.tensor_tensor(
                            out=actT[:, j, t2 * 512:(t2 + 1) * 512], in0=pb[ub][:], in1=sg[:], op=ALU.mult),
                            reads=[f"pb{ub}", f"sg{c}"], writes=[f"a{j}_{t2}"])

            def do_down(half):
                for oc in range(8):
                    wi = st["wd"] % 2
                    st["wd"] += 1
                    wd = wds[wi]
                    fw.dma(sp, f"wd{wi}", wd[:].rearrange("p a b -> p (a b)"), wd_s[l, f, oc], reads=["scr"], writes=[f"wd{wi}"])
                    for t2 in range(2):
                        bank = 4 + st["d"] % 2
                        st["d"] += 1
                        tt = half * 2 + t2
                        fw.op(pe, [lambda e, j=j, wd=wd, bank=bank, t2=t2: e.matmul(
                            pb[bank][:], lhsT=wd[:, j, :], rhs=actT[:, j, t2 * 512:(t2 + 1) * 512], start=(j == 0), stop=(j == NJ - 1))
                            for j in range(NJ)], reads=[f"wd{wi}"] + [f"a{j}_{t2}" for j in range(NJ)], writes=[f"pb{bank}"])
                        fw.op(dve, lambda e, bank=bank, oc=oc, tt=tt: e.scalar_tensor_tensor(
                            out=xT[:, oc, tsl(tt)], in0=pb[bank][:], scalar=0.5, in1=xT[:, oc, tsl(tt)],
                            op0=ALU.mult, op1=ALU.add), reads=[f"pb{bank}", xk(oc, tt)], writes=[xk(oc, tt)])

            do_norm(0)
            do_gu(0)
            do_norm(1)
            do_down(0)
            do_gu(1)
            do_down(1)
            fw.barrier()

    def mixer(l):
        lam0 = _lam_init(l)
        with ExitStack() as _es:
            wbuf = _es.enter_context(SB("wbuf", [128, 8, 772], BF16))
            t0 = _es.enter_context(SB("qk0", [128, S], BF16))
            t1 = _es.enter_context(SB("qk1", [128, S], BF16))
            t2_ = _es.enter_context(SB("qk2", [128, S], BF16))
            t3 = _es.enter_context(SB("qk3", [128, S], BF16))
            t4 = _es.enter_context(SB("qk4", [128, S], BF16))
            t5 = _es.enter_context(SB("qk5", [128, S], BF16))
            t6 = _es.enter_context(SB("qk6", [128, S], BF16))
            t7 = _es.enter_context(SB("qk7", [128, S], BF16))
            Vt = _es.enter_context(SB("Vt", [128, 16, 4, 128], BF16))
            oT0 = _es.enter_context(SB("oT0", [128, S], BF16))
            oT1 = _es.enter_context(SB("oT1", [128, S], BF16))
            Pts = [_es.enter_context(SB(f"Pt{i}", [128, 512], BF16)) for i in range(6)]
            rb0 = _es.enter_context(SB("rb0", [128, 512], F32))
            rb1 = _es.enter_context(SB("rb1", [128, 512], F32))
            qk = (t0, t1, t2_, t3, t4, t5, t6, t7)
            oTs = (oT0, oT1)
            rbs = (rb0, rb1)
            ctr = {"s": 0, "p": 0, "o": 0, "pj": 0, "v": 0, "w": 0, "r": 0}

            def next_s():
                ctr["s"] += 1
                return ctr["s"] % 3

            def next_p():
                ctr["p"] += 1
                return ctr["p"] % 3

            def load_win(m_):
                base_ = (0, 772, 1540, 2308)[m_]
                ncols_ = 772 if m_ == 0 else 768
                fw.dma(sp, "wb", wbuf[:, :, 0:ncols_],
                       bass.AP(win_s, l * D * INW + base_, [[INW, 128], [128 * INW, 8], [1, ncols_]]),
                       reads=["scr"], writes=["wbuf"])

            load_win(MIXERS[0])
            for tt in range(4):
                norm_to_h(l * 3 + 1, tt, tt * 512)
            wo = bass.AP(sq, 0, [[8 * 512, 128], [1024, 2], [1, 1024]])
            for mi, m in enumerate(MIXERS):
                qscale = (0.125, 0.125, 32 ** -0.5, 0.125)[m]
                fw.dma(sp, "wo", wo, bass.AP(wout_s, l * D * D + m * 256 * D, [[D, 128], [128 * D, 2], [1, D]]),
                       reads=["scr"], writes=["sq"])
                nxt = MIXERS[mi + 1] if mi + 1 < len(MIXERS) else None
                fw.op(pool, lambda e: e.memset(Vt[:, :, :, 64:128], 1.0), writes=["Vt"])
                def vproj():
                    for tc in range(16):
                        bank = 3 + ctr["v"] % 2
                        ctr["v"] += 1
                        fw.op(pe, [lambda e, kc=kc, tc=tc, bank=bank: e.matmul(
                            pb[bank][:, 0:256], lhsT=hT[:, kc, tc * 128:(tc + 1) * 128], rhs=wbuf[:, kc, 512:768],
                            start=(kc == 0), stop=(kc == 7)) for kc in range(8)],
                            reads=["wbuf", f"h{tc // 4}"], writes=[f"pb{bank}"])
                        src = pb[bank][:, 0:256].rearrange("p (h c) -> p h c", c=64)
                        if tc % 2 == 0:
                            fw.op(act, lambda e, src=src, tc=tc: e.activation(out=Vt[:, tc, :, 0:64], in_=src, func=AF.Copy),
                                  reads=[f"pb{bank}"], writes=["Vt"])
                        else:
                            fw.op(dve, lambda e, src=src, tc=tc: e.tensor_copy(out=Vt[:, tc, :, 0:64], in_=src),
                                  reads=[f"pb{bank}"], writes=["Vt"])
                    if nxt is not None:
                        load_win(nxt)
                def qkproj(act_only=False):
                    for which in range(2):
                        for pair in range(2):
                            c0 = which * 256 + pair * 128
                            for tt in range(4):
                                bank = ctr["pj"] % 2
                                ctr["pj"] += 1
                                fw.op(pe, [lambda e, kc=kc, c0=c0, tt=tt, bank=bank: e.matmul(
                                    pb[bank][:], lhsT=wbuf[:, kc, c0:c0 + 128], rhs=hT[:, kc, tsl(tt)],
                                    start=(kc == 0), stop=(kc == 7)) for kc in range(8)],
                                    reads=["wbuf", f"h{tt}"], writes=[f"pb{bank}"])
                                sc = qscale if which == 0 else 1.0
                                if m != 2:
                                    pieces = [(0, 64, qk[which * 4 + 2 * pair], 0), (64, 64, qk[which * 4 + 2 * pair + 1], 0)]
                                else:
                                    ta, tb = qk[which * 4 + 2 * pair], qk[which * 4 + 2 * pair + 1]
                                    pieces = [(0, 32, ta, 0), (32, 32, ta, 64), (64, 32, tb, 0), (96, 32, tb, 64)]
                                for pi, (r0, nr, dst, d0) in enumerate(pieces):
                                    hidx = qk.index(dst)
                                    if pi % 2 == 0 or act_only:
                                        fw.op(act, lambda e, r0=r0, nr=nr, dst=dst, d0=d0, bank=bank, tt=tt, sc=sc: e.activation(
                                            out=dst[d0:d0 + nr, tsl(tt)], in_=pb[bank][r0:r0 + nr, :], func=AF.Copy, scale=sc),
                                            reads=[f"pb{bank}"], writes=[f"qk{hidx}"])
                                    else:
                                        fw.op(dve, lambda e, r0=r0, nr=nr, dst=dst, d0=d0, bank=bank, tt=tt, sc=sc: e.tensor_scalar(
                                            out=dst[d0:d0 + nr, tsl(tt)], in0=pb[bank][r0:r0 + nr, :], scalar1=sc, scalar2=None,
                                            op0=ALU.mult), reads=[f"pb{bank}"], writes=[f"qk{hidx}"])

                def finalize_softmax(h, qt, ob):
                    ri = ctr["r"] % 2
                    ctr["r"] += 1
                    rb = rbs[ri]
                    fw.op(act, lambda e: e.activation(out=rb[64:128, :], in_=pb[ob][64:128, :], func=AF.Ln),
                          reads=[f"pb{ob}"], writes=[f"rb{ri}"])
                    fw.op(act, lambda e: e.activation(out=rb[64:128, :], in_=rb[64:128, :], func=AF.Exp, scale=-1.0),
                          reads=[f"rb{ri}"], writes=[f"rb{ri}"])
                    dst = oTs[h // 2][(h % 2) * 64:(h % 2) * 64 + 64, tsl(qt)]
                    fw.op(dve, lambda e: e.tensor_tensor(out=dst, in0=pb[ob][0:64, :], in1=rb[64:128, :], op=ALU.mult),
                          reads=[f"pb{ob}", f"rb{ri}"], writes=[f"oT{h // 2}"])

                def run_pipe(steps, LA=2, mid=None):
                    n = len(steps)
                    for i in range(n + LA):
                        if i == LA and mid is not None:
                            mid()
                        if i < n:
                            steps[i][0]()
                        if i >= LA:
                            steps[i - LA][1]()

                def pv_back(h, j, n, c0, ob, first, last, pi, fin):
                    Pt = Pts[pi]

                    def back():
                        fw.op(pe, lambda e: e.matmul(
                            pb[ob][:, c0:c0 + n], lhsT=Vt[:, j, h, :], rhs=Pt[:, 0:n], start=first, stop=last,
                            skip_group_check=True), reads=["Vt", f"Pt{pi}"], writes=[f"pb{ob}"])
                        if last and fin is not None:
                            fin()
                    return back

                def attn_A(dtmps):
                    steps = []
                    gi = 0
                    for h in range(4):
                        qa, ka = qk[h], qk[4 + h]
                        for qt in range(4):
                            ob = 3 + ctr["o"] % 2
                            ctr["o"] += 1
                            jmax = 4 * qt + 3
                            for j in range(jmax + 1):
                                q0 = max(j, 4 * qt) * 128
                                n = (4 * qt + 4) * 128 - q0
                                c0 = q0 - qt * 512
                                sb = (0, 1, 2, 5, 6)[gi % 5]
                                pi = gi % 6
                                di = gi % 2
                                gi += 1
                                diag = j >= 4 * qt

                                def front(h=h, qa=qa, ka=ka, j=j, q0=q0, n=n, sb=sb, pi=pi, di=di, diag=diag):
                                    Pt = Pts[pi]
                                    fw.op(pe, lambda e: e.matmul(
                                        pb[sb][:, 0:n], lhsT=ka[0:70, j * 128:(j + 1) * 128], rhs=qa[0:70, q0:q0 + n],
                                        start=True, stop=True),
                                        reads=[f"qk{h}", f"qk{4 + h}"] + [f"qa{h}_{t_}" for t_ in range(4)] + [f"qa{4 + h}_{t_}" for t_ in range(4)],
                                        writes=[f"pb{sb}"])
                                    if diag:
                                        dtmp = dtmps[di]
                                        fw.op(dve, lambda e: e.tensor_tensor(out=dtmp[:, 0:128], in0=pb[sb][:, 0:128], in1=maskc[:, 4, :], op=ALU.add),
                                              reads=[f"pb{sb}", "c_maskc"], writes=[f"dtmp{di}"])
                                        fw.op(act, lambda e: e.activation(out=Pt[:, 0:128], in_=dtmp[:, 0:128], func=AF.Exp),
                                              reads=[f"dtmp{di}"], writes=[f"Pt{pi}"])
                                        if n > 128:
                                            fw.op(act, lambda e: e.activation(out=Pt[:, 128:n], in_=pb[sb][:, 128:n], func=AF.Exp),
                                                  reads=[f"pb{sb}"], writes=[f"Pt{pi}"])
                                    else:
                                        fw.op(act, lambda e: e.activation(out=Pt[:, 0:n], in_=pb[sb][:, 0:n], func=AF.Exp),
                                              reads=[f"pb{sb}"], writes=[f"Pt{pi}"])
                                fin = (lambda h=h, qt=qt, ob=ob: finalize_softmax(h, qt, ob))
                                steps.append((front, pv_back(h, j, n, c0, ob, j == 0, j == jmax, pi, fin)))
                    run_pipe(steps, LA=5, mid=vproj)

                def attn_B(EB, tbs):
                    steps = []
                    gi = 0
                    for h in range(4):
                        qa, ka = qk[h], qk[4 + h]
                        for qt in range(4):
                            ob = 3 + ctr["o"] % 2
                            ctr["o"] += 1
                            js = [4 * qt] + [j for j in range(max(0, 4 * qt - 4), 4 * qt + 4) if j != 4 * qt]
                            for idx, j in enumerate(js):
                                i0 = max(j, 4 * qt)
                                i1 = min(j + 4, 4 * qt + 3)
                                q0 = i0 * 128
                                n = (i1 - i0 + 1) * 128
                                eoff = (i0 - j) * 128
                                c0 = q0 - qt * 512
                                sb = (0, 1, 2, 5, 6)[gi % 5]
                                pi = gi % 6
                                ti = gi % 2
                                gi += 1

                                def front(h=h, qa=qa, ka=ka, j=j, q0=q0, n=n, sb=sb, pi=pi, ti=ti, eoff=eoff):
                                    Pt = Pts[pi]
                                    tb = tbs[ti]
                                    fw.op(pe, lambda e: e.matmul(
                                        pb[sb][:, 0:n], lhsT=ka[:, j * 128:(j + 1) * 128], rhs=qa[:, q0:q0 + n],
                                        start=True, stop=True), reads=[f"qk{h}", f"qk{4 + h}"], writes=[f"pb{sb}"])
                                    fw.op(act, lambda e: e.activation(out=tb[:, 0:n], in_=pb[sb][:, 0:n], func=AF.Exp),
                                          reads=[f"pb{sb}"], writes=[f"tb{ti}"])
                                    fw.op(dve, lambda e: e.tensor_tensor(
                                        out=Pt[:, 0:n], in0=tb[:, 0:n], in1=EB[:, h, eoff:eoff + n], op=ALU.mult),
                                        reads=[f"tb{ti}", "EB"], writes=[f"Pt{pi}"])
                                fin = (lambda h=h, qt=qt, ob=ob: finalize_softmax(h, qt, ob))
                                steps.append((front, pv_back(h, j, n, c0, ob, idx == 0, idx == len(js) - 1, pi, fin)))
                    run_pipe(steps, LA=5, mid=vproj)

                def attn_C(c1s, c2s, csq, cl):
                    steps = []
                    pend = []
                    gi = 0
                    tile_i = 0
                    H = slice(0, 64)
                    for h in range(4):
                        qa, ka = qk[h], qk[4 + h]
                        for qt in range(4):
                            obs = (3, 4) if tile_i % 2 == 0 else (5, 6)
                            c1, c2 = c1s[tile_i % 2], c2s[tile_i % 2]
                            ck = tile_i % 2
                            tile_i += 1
                            jmax = 4 * qt + 3

                            def fin(h=h, qt=qt, obs=obs, c1=c1, c2=c2, ck=ck):
                                o1, o2 = obs
                                for (ob, rb, rk) in ((o1, rb0, "rb0"), (o2, rb1, "rb1")):
                                    fw.op(act, lambda e, ob=ob, rb=rb: e.activation(out=rb[64:128, :], in_=pb[ob][64:128, :], func=AF.Ln),
                                          reads=[f"pb{ob}"], writes=[rk])
                                    fw.op(act, lambda e, rb=rb: e.activation(out=rb[64:128, :], in_=rb[64:128, :], func=AF.Exp, scale=-1.0),
                                          reads=[rk], writes=[rk])
                                fw.op(dve, lambda e: e.tensor_tensor(out=c1[H, :], in0=pb[o1][H, :], in1=rb0[64:128, :], op=ALU.mult),
                                      reads=[f"pb{o1}", "rb0"], writes=[f"c1{ck}"])
                                fw.op(dve, lambda e: e.tensor_tensor(out=c2[H, :], in0=pb[o2][H, :], in1=rb1[64:128, :], op=ALU.mult),
                                      reads=[f"pb{o2}", "rb1"], writes=[f"c2{ck}"])
                                fw.op(dve, lambda e: e.scalar_tensor_tensor(out=c1[H, :], in0=c2[H, :], scalar=lamc[H, l:l + 1], in1=c1[H, :],
                                                                            op0=ALU.mult, op1=ALU.add),
                                      reads=[f"c1{ck}", f"c2{ck}", "c_lamc"], writes=[f"c1{ck}"])
                                fw.op(act, lambda e: e.activation(out=csq[H, :], in_=c1[H, :], func=AF.Square), reads=[f"c1{ck}"], writes=["csq"])

                            def fin_b(h=h, qt=qt, c1=c1, ck=ck):
                                fw.op(pe, lambda e: e.matmul(pb[7][H, :], lhsT=masks[H, 4, 0:64], rhs=csq[H, :], start=True, stop=True),
                                      reads=["csq", "c_masks"], writes=["pb7"])
                                fw.op(act, lambda e: e.activation(out=cl[H, :], in_=pb[7][H, :], func=AF.Ln, bias=cc[H, 0:1], scale=1.0 / 64),
                                      reads=["pb7", "c_cc"], writes=["cl"])
                                fw.op(act, lambda e: e.activation(out=cl[H, :], in_=cl[H, :], func=AF.Exp, scale=-0.5), reads=["cl"], writes=["cl"])
                                dst = oTs[h // 2][(h % 2) * 64:(h % 2) * 64 + 64, tsl(qt)]
                                fw.op(dve, lambda e: e.scalar_tensor_tensor(out=dst, in0=c1[H, :], scalar=1.0 - lam0, in1=cl[H, :],
                                                                            op0=ALU.mult, op1=ALU.mult),
                                      reads=[f"c1{ck}", "cl"], writes=[f"oT{h // 2}"])

                            for j in range(jmax + 1):
                                q0 = max(j, 4 * qt) * 128
                                n = (4 * qt + 4) * 128 - q0
                                c0 = q0 - qt * 512
                                diag = j >= 4 * qt
                                sbs = ((0, 1), (2, 7))[gi % 2]
                                pis = ((2 * gi) % 6, (2 * gi + 1) % 6)
                                gi += 1

                                def front(h=h, qa=qa, ka=ka, j=j, q0=q0, n=n, sbs=sbs, pis=pis, diag=diag):
                                    fw.op(pe, [lambda e, mp=mp: e.matmul(
                                        pb[sbs[mp]][:, 0:n], lhsT=ka[mp * 64:mp * 64 + 36, j * 128:(j + 1) * 128],
                                        rhs=qa[mp * 64:mp * 64 + 36, q0:q0 + n], start=True, stop=True) for mp in range(2)],
                                        reads=[f"qk{h}", f"qk{4 + h}"], writes=[f"pb{sbs[0]}", f"pb{sbs[1]}"])
                                    for mp in range(2):
                                        Pt = Pts[pis[mp]]
                                        fw.op(act, lambda e, Pt=Pt, mp=mp: e.activation(out=Pt[:, 0:n], in_=pb[sbs[mp]][:, 0:n], func=AF.Exp),
                                              reads=[f"pb{sbs[mp]}"], writes=[f"Pt{pis[mp]}"])
                                        if diag:
                                            fw.op(dve, lambda e, Pt=Pt: e.tensor_tensor(out=Pt[:, 0:128], in0=Pt[:, 0:128], in1=maskc[:, h, :], op=ALU.mult),
                                                  reads=[f"Pt{pis[mp]}", "c_maskc"], writes=[f"Pt{pis[mp]}"])

                                def back(h=h, j=j, n=n, c0=c0, obs=obs, pis=pis, jmax=jmax, fin=fin, fin_b=fin_b):
                                    for mp in range(2):
                                        Pt = Pts[pis[mp]]
                                        fw.op(pe, lambda e, Pt=Pt, mp=mp: e.matmul(
                                            pb[obs[mp]][:, c0:c0 + n], lhsT=Vt[:, j, h, :], rhs=Pt[:, 0:n], start=(j == 0), stop=(j == jmax),
                                            skip_group_check=True), reads=["Vt", f"Pt{pis[mp]}"], writes=[f"pb{obs[mp]}"])
                                    for item in list(pend):
                                        item[0] -= 1
                                        if item[0] <= 0:
                                            pend.remove(item)
                                            item[1]()
                                    if j == jmax:
                                        fin()
                                        pend.append([2, fin_b])
                                steps.append((front, back))
                    run_pipe(steps, LA=2, mid=vproj)
                    for item in pend:
                        item[1]()

                def attn_D(e32s, l32s, lbs, t1s):
                    st = []
                    tile_i = 0
                    for h in range(4):
                        for qt in range(4):
                            ob = 3 + tile_i % 2
                            tile_i += 1
                            jmax = 4 * qt + 3
                            for k, j in enumerate(range(jmax, -1, -1)):
                                q0 = max(j, 4 * qt) * 128
                                st.append(dict(h=h, qt=qt, ob=ob, k=k, K=jmax + 1, j=j, q0=q0, n=(4 * qt + 4) * 128 - q0,
                                               c0=q0 - qt * 512, diag=(j >= 4 * qt)))
                    G = len(st)
                    SBK = (0, 1, 2, 7)

                    def F1(g):
                        s = st[g]
                        h, j, q0, n = s["h"], s["j"], s["q0"], s["n"]
                        sb = SBK[g % 4]
                        e32, l32, lb = e32s[g % 2], l32s[g % 3], lbs[g % 3]
                        qa, ka = qk[h], qk[4 + h]
                        fw.op(pe, lambda e: e.matmul(
                            pb[sb][:, 0:n], lhsT=ka[:, j * 128:(j + 1) * 128], rhs=qa[:, q0:q0 + n],
                            start=True, stop=True), reads=[f"qk{h}", f"qk{4 + h}"], writes=[f"pb{sb}"])
                        fw.op(act, lambda e: e.activation(out=e32[:, 0:n], in_=pb[sb][:, 0:n], func=AF.Exp),
                              reads=[f"pb{sb}"], writes=[f"e32{g % 2}"])
                        fw.op(act, lambda e: e.activation(out=l32[:, 0:n], in_=e32[:, 0:n], func=AF.Ln, bias=cc[:, 1:2], scale=1.0),
                              reads=[f"e32{g % 2}", "c_cc"], writes=[f"l32{g % 3}"])

                    def F1b(g):
                        s = st[g]
                        n = s["n"]
                        l32, lb = l32s[g % 3], lbs[g % 3]
                        fw.op(dve, lambda e: e.tensor_copy(out=lb[:, 0:n], in_=l32[:, 0:n]),
                              reads=[f"l32{g % 3}"], writes=[f"lb{g % 3}"])
                        if s["diag"]:
                            fw.op(pool, lambda e: e.tensor_tensor(out=lb[:, 0:128], in0=lb[:, 0:128], in1=mD, op=ALU.mult),
                                  reads=[f"lb{g % 3}", "c_masks"], writes=[f"lb{g % 3}"])

                    def F3(g):
                        s = st[g]
                        n, c0, k = s["n"], s["c0"], s["k"]
                        sb = SBK[g % 4]
                        l32, t1, Pt = l32s[g % 3], t1s[g % 2], Pts[g % 4]
                        rbk = 5 + k % 2
                        fw.op(dve, lambda e: e.tensor_tensor(out=t1[:, 0:n], in0=pb[sb][:, 0:n], in1=l32[:, 0:n], op=ALU.subtract),
                              reads=[f"pb{sb}", f"l32{g % 3}"], writes=[f"t1{g % 2}"])
                        fw.op(dve, lambda e: e.tensor_tensor(out=t1[:, 0:n], in0=t1[:, 0:n], in1=pb[rbk][:, c0:c0 + n], op=ALU.subtract),
                              reads=[f"t1{g % 2}", f"pb{rbk}"], writes=[f"t1{g % 2}"])
                        fw.op(act, lambda e: e.activation(out=Pt[:, 0:n], in_=t1[:, 0:n], func=AF.Exp),
                              reads=[f"t1{g % 2}"], writes=[f"Pt{g % 4}"])
                        if s["diag"]:
                            fw.op(pool, lambda e: e.tensor_tensor(out=Pt[:, 0:128], in0=Pt[:, 0:128], in1=mD, op=ALU.mult),
                                  reads=[f"Pt{g % 4}", "c_masks"], writes=[f"Pt{g % 4}"])

                    def PE_U(g):
                        s = st[g]
                        n, c0, k = s["n"], s["c0"], s["k"]
                        lb = lbs[g % 3]
                        b, bo = 5 + k % 2, 5 + (k + 1) % 2
                        fw.op(pe, lambda e: e.matmul(pb[b][:, c0:c0 + n], lhsT=mU, rhs=lb[:, 0:n], start=(k == 0), stop=False,
                                                     skip_group_check=True),
                              reads=[f"lb{g % 3}", "c_masks"], writes=[f"pb{b}"])
                        if k == 0:
                            fw.op(pe, lambda e: e.matmul(pb[bo][:, c0:c0 + n], lhsT=ones_b, rhs=lb[:, 0:n], start=True, stop=False,
                                                         skip_group_check=True),
                                  reads=[f"lb{g % 3}", "c_masks"], writes=[f"pb{bo}"])

                    def PE_fix(g):
                        s = st[g]
                        k, K = s["k"], s["K"]
                        if k > K - 3:
                            return
                        s1 = st[g + 1]
                        b = 5 + k % 2
                        lb0, lb1 = lbs[g % 3], lbs[(g + 1) % 3]
                        fw.op(pe, [lambda e: e.matmul(pb[b][:, s["c0"]:s["c0"] + s["n"]], lhsT=mLI, rhs=lb0[:, 0:s["n"]], start=False, stop=False,
                                                      skip_group_check=True),
                                   lambda e: e.matmul(pb[b][:, s1["c0"]:s1["c0"] + s1["n"]], lhsT=ones_b, rhs=lb1[:, 0:s1["n"]], start=False, stop=False,
                                                      skip_group_check=True)],
                              reads=[f"lb{g % 3}", f"lb{(g + 1) % 3}", "c_masks"], writes=[f"pb{b}"])

                    def PE_PV(g):
                        s = st[g]
                        h, qt, ob, k, K, j, n, c0 = s["h"], s["qt"], s["ob"], s["k"], s["K"], s["j"], s["n"], s["c0"]
                        Pt = Pts[g % 4]
                        fw.op(pe, lambda e: e.matmul(
                            pb[ob][:, c0:c0 + n], lhsT=Vt[:, j, h, :], rhs=Pt[:, 0:n], start=(k == 0), stop=(k == K - 1),
                            skip_group_check=True), reads=["Vt", f"Pt{g % 4}"], writes=[f"pb{ob}"])
                        if k == K - 1:
                            dst = oTs[h // 2][(h % 2) * 64:(h % 2) * 64 + 64, tsl(qt)]
                            fw.op(act, lambda e: e.activation(out=dst, in_=pb[ob][0:64, :], func=AF.Copy),
                                  reads=[f"pb{ob}"], writes=[f"oT{h // 2}"])

                    for it in range(-2, G + 2):
                        if it == 0:
                            vproj()
                        if 0 <= it - 1 < G:
                            PE_fix(it - 1)
                        if 0 <= it + 2 < G:
                            F1(it + 2)
                        if 0 <= it < G:
                            F3(it)
                        if 0 <= it + 2 < G:
                            F1b(it + 2)
                        if 0 <= it + 1 < G:
                            PE_U(it + 1)
                        if 0 <= it - 2 < G:
                            PE_PV(it - 2)

                if m == 0:
                    def _attn():
                        with ExitStack() as _es:
                            FA = _es.enter_context(SB("FA", [128, 512], F32))
                            ACC = _es.enter_context(SB("ACC", [128, 512], F32))
                            RR = _es.enter_context(SB("RR", [128, 512], F32))
                            AQ = _es.enter_context(SB("AQ", [128, 512], F32))
                            HB = _es.enter_context(SB("HB", [128, 512], BF16))
                            MB = _es.enter_context(SB("MB", [128, 512], BF16))
                            LB = _es.enter_context(SB("LB", [128, 512], BF16))
                            carry = _es.enter_context(SB("carry", [128, 2], F32))
                            dtmps = [_es.enter_context(SB(f"dtmp{i}", [128, 128], F32)) for i in range(2)]
                            wfa = _es.enter_context(SB("wfa", [128, 8, 4, 6], BF16))
                            STs = [[_es.enter_context(SB(f"ST{w}{i}", [128, 512], BF16)) for i in range(2)] for w in range(2)]
                            fw.op(dve, lambda e: e.tensor_copy(
                                out=wfa[:], in_=bass.AP(wbuf, 768, [[8 * 772, 128], [772, 8], [1, 4], [0, 6]])),
                                reads=["wbuf"], writes=["wfa"])
                            R = slice(0, 24)
                            for tt in range(4):
                                bank = (5, 6, 7, 4)[tt]
                                fw.op(pe, [lambda e, kc=kc, tt=tt, bank=bank: e.matmul(
                                    pb[bank][R, :], lhsT=wfa[:, kc, :, :].rearrange("p h r -> p (h r)"), rhs=hT[:, kc, tsl(tt)],
                                    start=(kc == 0), stop=(kc == 7)) for kc in range(8)],
                                    reads=["wfa", f"h{tt}"], writes=[f"pb{bank}"])
                                fw.op(dve, lambda e, bank=bank: e.tensor_scalar(
                                    out=FA[R, :], in0=pb[bank][R, :], scalar1=bfb[R, l:l + 1], scalar2=None,
                                    op0=ALU.add), reads=[f"pb{bank}", "c_bfb"], writes=["FA"])
                                fw.op(act, lambda e: e.activation(out=FA[R, :], in_=FA[R, :], func=AF.Exp, scale=-1.0),
                                      reads=["FA"], writes=["FA"])
                                fw.op(act, lambda e: e.activation(out=FA[R, :], in_=FA[R, :], func=AF.Ln, bias=cc[R, 1:2], scale=1.0),
                                      reads=["FA", "c_cc"], writes=["FA"])
                                ini = 0.0 if tt == 0 else carry[R, 0:1]
                                fw.op(dve, lambda e, ini=ini: e.tensor_tensor_scan(out=ACC[R, :], data0=FA[R, :], data1=FA[R, :], initial=ini,
                                                                                   op0=ALU.add, op1=ALU.max),
                                      reads=["FA", "carry"], writes=["ACC"])
                                fw.op(dve, lambda e: e.tensor_copy(out=carry[R, 0:1], in_=ACC[R, 511:512]), reads=["ACC"], writes=["carry"])
                                fw.op(dve, lambda e: e.tensor_copy(out=HB[R, :], in_=ACC[R, :]), reads=["ACC"], writes=["HB"])
                                fw.op(dve, lambda e: e.tensor_tensor(out=RR[R, :], in0=ACC[R, :], in1=HB[R, :], op=ALU.subtract),
                                      reads=["ACC", "HB"], writes=["RR"])
                                fw.op(dve, lambda e: e.tensor_copy(out=MB[R, :], in_=RR[R, :]), reads=["RR"], writes=["MB"])
                                fw.op(dve, lambda e: e.tensor_tensor(out=ACC[R, :], in0=RR[R, :], in1=MB[R, :], op=ALU.subtract),
                                      reads=["RR", "MB"], writes=["ACC"])
                                fw.op(dve, lambda e: e.tensor_copy(out=LB[R, :], in_=ACC[R, :]), reads=["ACC"], writes=["LB"])
                                for w_, c0 in ((0, 2), (1, 6)):
                                    ST = STs[w_][tt % 2]
                                    stk = f"ST{w_}{tt % 2}"
                                    fw.op(dve, lambda e, c0=c0: e.tensor_scalar(
                                        out=AQ[R, :], in0=HB[R, :], scalar1=cc[R, c0:c0 + 1], scalar2=cc[R, c0 + 3:c0 + 4],
                                        op0=ALU.mult, op1=ALU.add), reads=["HB", "c_cc"], writes=["AQ"])
                                    fw.op(dve, lambda e, c0=c0: e.scalar_tensor_tensor(
                                        out=RR[R, :], in0=MB[R, :], scalar=cc[R, c0 + 1:c0 + 2], in1=AQ[R, :],
                                        op0=ALU.mult, op1=ALU.add), reads=["MB", "AQ", "c_cc"], writes=["RR"])
                                    fw.op(dve, lambda e, c0=c0, ST=ST: e.scalar_tensor_tensor(
                                        out=ST[R, :], in0=LB[R, :], scalar=cc[R, c0 + 2:c0 + 3], in1=RR[R, :],
                                        op0=ALU.mult, op1=ALU.add), reads=["LB", "RR", "c_cc"], writes=[stk])
                                    for h in range(4):
                                        idx = w_ * 4 + h
                                        fw.dma(sp, f"aug{idx}_{tt % 2}", qk[idx][64:70, tsl(tt)], ST[h * 6:(h + 1) * 6, :],
                                               reads=[stk], writes=[f"qa{idx}_{tt}"])
                            qkproj(act_only=True)
                            attn_A(dtmps)
                            fw.barrier()
                    _attn()
                if m == 1:
                    def _attn():
                        with ExitStack() as _es:
                            for idx_ in range(8):
                                fw.op(pool, lambda e, idx_=idx_: e.memset(qk[idx_][64:128, :], 0.0), writes=[f"qk{idx_}"])
                            qkproj()
                            EB = _es.enter_context(SB("EB", [128, 4, 640], F32))
                            visb = _es.enter_context(SB("sb_visb", [128, 640], BF16))
                            tbs = [_es.enter_context(SB(f"tb{i}", [128, 512], F32)) for i in range(2)]
                            fw.dma(sp, "eb", EB[:], relb_d[l], writes=["EB"])
                            fw.dma(sp, "eb2", visb[:], visb_d.ap(), writes=["visb"])
                            fw.op(act, lambda e: e.activation(out=EB[:], in_=EB[:], func=AF.Exp), reads=["EB"], writes=["EB"])
                            for h in range(4):
                                fw.op(dve, lambda e, h=h: e.tensor_tensor(out=EB[:, h, :], in0=EB[:, h, :], in1=visb[:], op=ALU.mult),
                                      reads=["EB", "visb"], writes=["EB"])
                            attn_B(EB, tbs)
                            fw.barrier()
                    _attn()
                if m == 2:
                    def _attn():
                        with ExitStack() as _es:
                            qkproj()
                            c1s = [_es.enter_context(SB(f"c1{i}", [128, 512], F32)) for i in range(2)]
                            c2s = [_es.enter_context(SB(f"c2{i}", [128, 512], F32)) for i in range(2)]
                            csq = _es.enter_context(SB("csq", [128, 512], BF16))
                            cl = _es.enter_context(SB("cl", [128, 512], F32))
                            for h in range(4):
                                for which in range(2):
                                    dst = qk[which * 4 + h]
                                    for mp in range(2):
                                        fw.dma(sp, f"cau{which * 4 + h}", dst[mp * 64 + 32:mp * 64 + 36, :], caug_d[h, which],
                                               writes=[f"qk{which * 4 + h}"])
                            attn_C(c1s, c2s, csq, cl)
                            fw.barrier()
                    _attn()
                if m == 3:
                    def _attn():
                        with ExitStack() as _es:
                            for idx_ in range(8):
                                fw.op(pool, lambda e, idx_=idx_: e.memset(qk[idx_][64:128, :], 0.0), writes=[f"qk{idx_}"])
                            qkproj()
                            e32s = [_es.enter_context(SB(f"e32{i}", [128, 512], F32)) for i in range(2)]
                            l32s = [_es.enter_context(SB(f"l32{i}", [128, 512], F32)) for i in range(3)]
                            lbs = [_es.enter_context(SB(f"lb{i}", [128, 512], BF16)) for i in range(3)]
                            t1s = [_es.enter_context(SB(f"t1{i}", [128, 512], F32)) for i in range(2)]
                            attn_D(e32s, l32s, lbs, t1s)
                            fw.barrier()
                    _attn()

                for oc in range(8):
                    for tt in range(4):
                        bank = 5 + ctr["w"] % 2
                        ctr["w"] += 1
                        fw.op(pe, [lambda e, pr=pr, oc=oc, tt=tt, bank=bank: e.matmul(
                            pb[bank][:], lhsT=wo[:, pr, oc * 128:(oc + 1) * 128], rhs=oTs[pr][:, tsl(tt)],
                            start=(pr == 0), stop=(pr == 1)) for pr in range(2)],
                            reads=["sq", "oT0", "oT1"], writes=[f"pb{bank}"])
                        fw.op(dve, lambda e, oc=oc, tt=tt, bank=bank: e.tensor_tensor(
                            out=xT[:, oc, tsl(tt)], in0=pb[bank][:], in1=xT[:, oc, tsl(tt)], op=ALU.add),
                            reads=[f"pb{bank}", xk(oc, tt)], writes=[xk(oc, tt)])
                if nxt is None:
                    fw.barrier()

    if stages is None:
        stages = []
        for l in range(NL):
            stages += [("ffn", l, 0), ("mix", l), ("ffn", l, 1)]
    xfer(None, 0, final_norm)
    for b in range(nseq):
        for stg in stages:
            if stg[0] == "ffn":
                ffn(stg[1], stg[2])
            else:
                mixer(stg[1])
        xfer(b, b + 1 if b + 1 < nseq else None, final_norm)
    nc._fw_stats = (fw.n_inst, fw.n_wait)
    return nc


def _host_inputs(inputs):
    f32 = np.float32
    g = np.stack([inputs["g_ffn1"][0], inputs["g_mix"][0], inputs["g_ffn2"][0],
                  inputs["g_ffn1"][1], inputs["g_mix"][1], inputs["g_ffn2"][1], inputs["g_final"]], 0).astype(f32)
    gall = np.ascontiguousarray(g.reshape(7, 8, 128).transpose(2, 0, 1).reshape(128, 56))
    bfb = np.zeros((128, 2), f32)
    bfb[0:24, :] = np.repeat(inputs["b_f"].astype(f32).T, 6, axis=0)
    dlb = np.ascontiguousarray(np.broadcast_to(inputs["diff_lambda"].astype(f32).reshape(1, 256), (128, 256)))
    kk = np.arange(128)[:, None]
    qq = np.arange(640)[None, :]
    ridx = np.clip(qq - kk, -256, 256) + 256
    relb = np.ascontiguousarray(inputs["rel_bias"].astype(f32)[:, :, ridx].transpose(0, 2, 1, 3))
    m = dict(_consts())
    m.update(gall=gall, bfb=bfb, dlb=dlb, relb=relb)
    for k in ("ffn1_w_gu", "ffn2_w_gu", "ffn1_w_down", "ffn2_w_down", "w_in", "w_out"):
        m[k] = np.ascontiguousarray(inputs[k], dtype=f32)
    return m


def kernel(**inputs):
    x = np.ascontiguousarray(inputs["x"], dtype=np.float32)
    shared = _host_inputs(inputs)
    nc = build()
    in_maps = []
    for c in range(NCORES):
        d = dict(shared)
        d["x"] = x[c * NSEQ:(c + 1) * NSEQ]
        in_maps.append(d)
    res = run_bass_kernel_spmd(nc, in_maps, core_ids=list(range(NCORES)))
    return np.concatenate([np.asarray(r["y"], dtype=np.float32) for r in res.results], axis=0)
```

```python
import math
from contextlib import ExitStack
import numpy as np
import ml_dtypes
import concourse.bass as bass
import concourse.mybir as mybir
from concourse.bass_utils import run_bass_kernel_spmd

F32 = mybir.dt.float32
BF16 = mybir.dt.bfloat16
AF = mybir.ActivationFunctionType
ALU = mybir.AluOpType

S = 2048
D = 1024
DFF = 2816
NJ = 22
INW = 3076
NL = 2
EPS = 1e-6
EPOCH = 30000
NCORES = 8
NSEQ = 4
MIXERS = (0, 1, 2, 3)


class _Eng:
    def __init__(self, fw, name, handle, nsem):
        self.name = name
        self.h = handle
        self.sems = [fw.nc.alloc_semaphore(f"s_{name}_{i}") for i in range(nsem)]
        self.count = 0
        self.known = {}


class FW:
    def __init__(self, nc, nsem=4):
        self.nc = nc
        self.pe = _Eng(self, "pe", nc.tensor, nsem)
        self.act = _Eng(self, "act", nc.scalar, nsem)
        self.dve = _Eng(self, "dve", nc.vector, nsem)
        self.pool = _Eng(self, "pool", nc.gpsimd, nsem)
        self.sp = _Eng(self, "sp", nc.sync, 1)
        self.engs = {e.name: e for e in (self.pe, self.act, self.dve, self.pool, self.sp)}
        self.lastw = {}
        self.readers = {}
        self.dma_sems = {}
        self.n_wait = 0
        self.n_inst = 0

    def _wait(self, eng, ev):
        if ev is None:
            return
        if ev[0] == 'e':
            src = self.engs[ev[1]]
            idx = ev[2]
            if src is eng and eng is self.pe:
                return
            if eng.known.get(src.name, 0) >= idx + 1:
                return
            eng.h.wait_ge(src.sems[idx // EPOCH], idx % EPOCH + 1)
            self.n_wait += 1
            eng.known[src.name] = idx + 1
        else:
            st, val = ev[1], ev[2]
            key = ('d', st)
            if eng.known.get(key, 0) >= val:
                return
            eng.h.wait_ge(self.dma_sems[st][0], val)
            self.n_wait += 1
            eng.known[key] = val

    def _deps(self, eng, reads, writes):
        for k in reads:
            self._wait(eng, self.lastw.get(k))
        for k in writes:
            self._wait(eng, self.lastw.get(k))
            for ev in self.readers.get(k, ()):
                self._wait(eng, ev)

    def _commit(self, ev, reads, writes):
        for k in reads:
            if not k.startswith("c_"):
                self.readers.setdefault(k, []).append(ev)
        for k in writes:
            self.lastw[k] = ev
            self.readers[k] = []

    def op(self, eng, fns, reads=(), writes=()):
        self._deps(eng, reads, writes)
        if not isinstance(fns, (list, tuple)):
            fns = [fns]
        inst = None
        for fn in fns:
            inst = fn(eng.h)
        idx = eng.count
        inst.then_inc(eng.sems[idx // EPOCH], 1)
        eng.count += 1
        self.n_inst += len(fns)
        self._commit(('e', eng.name, idx), reads, writes)

    def dma(self, queue, stream, out, in_, reads=(), writes=(), **kw):
        if stream not in self.dma_sems:
            self.dma_sems[stream] = [self.nc.alloc_semaphore(f"d_{stream}"), 0]
        self._deps(queue, reads, writes)
        ds = self.dma_sems[stream]
        queue.h.dma_start(out=out, in_=in_, **kw).then_inc(ds[0], 16)
        ds[1] += 16
        self.n_inst += 1
        self._commit(('d', stream, ds[1]), reads, writes)

    def barrier(self):
        evs = []
        for e in (self.pe, self.act, self.dve, self.pool):
            if e.count:
                evs.append(('e', e.name, e.count - 1))
        for st, (sem, val) in self.dma_sems.items():
            if val:
                evs.append(('d', st, val))
        for e in (self.pe, self.act, self.dve, self.pool, self.sp):
            for ev in evs:
                self._wait(e, ev)
        self.lastw = {}
        self.readers = {}


def _consts():
    c = {}
    c["ident"] = np.eye(128, dtype=np.float32)
    kk = np.arange(128)[:, None]
    qq = np.arange(128)[None, :]
    bf = ml_dtypes.bfloat16
    m = np.zeros((128, 5, 128), np.float32)
    m[:, 0, :] = (qq >= kk)
    m[:, 1, :] = (qq > kk)
    m[:, 2, :] = (kk > qq)
    m[:, 3, :] = (kk <= qq)
    m[:, 4, :] = 1.0
    c["masks"] = m.astype(bf)
    slopes = [2.0 ** (-8.0 * (i + 1) / 4) for i in range(4)]
    mc = np.zeros((128, 5, 128), np.float32)
    vis = ((kk // 64) <= (qq // 64))
    for h in range(4):
        corr = np.where(kk > qq, np.exp(-2.0 * slopes[h] * (kk - qq).astype(np.float64)), 1.0)
        mc[:, h, :] = np.where(vis, corr, 0.0)
    mc[:, 4, :] = np.where(qq >= kk, 0.0, -30000.0)
    c["maskc"] = mc
    t = np.arange(S)
    caug = np.zeros((4, 2, 4, S), np.float32)
    for h in range(4):
        sl = slopes[h]
        caug[h, 0, 0] = -sl * 64 * (t // 64)
        caug[h, 0, 1] = -sl * (t % 64)
        caug[h, 0, 2] = 1.0
        caug[h, 0, 3] = 1.0
        caug[h, 1, 0] = 1.0
        caug[h, 1, 1] = 1.0
        caug[h, 1, 2] = sl * 64 * (t // 64)
        caug[h, 1, 3] = sl * (t % 64)
    c["caug"] = caug.astype(bf)
    kk6 = np.arange(128)[:, None]
    qq6 = np.arange(640)[None, :]
    visb = ((kk6 // 64) <= (qq6 // 64)) & ((qq6 // 64) <= (kk6 // 64) + 8)
    c["visb"] = visb.astype(np.float32).astype(bf)
    cc = np.zeros((128, 16), np.float32)
    cc[:, 0] = EPS
    cc[:, 1] = 1.0
    for p in range(24):
        i = p % 6
        if i == 0: cc[p, 2] = -1.0
        if i == 1: cc[p, 3] = -1.0
        if i == 2: cc[p, 4] = -1.0
        if i >= 3: cc[p, 5] = 1.0
        if i == 3: cc[p, 6] = 1.0
        if i == 4: cc[p, 7] = 1.0
        if i == 5: cc[p, 8] = 1.0
        if i < 3: cc[p, 9] = 1.0
    c["cc"] = cc
    return c


def _lam_init(l):
    return 0.8 - 0.6 * math.exp(-0.3 * l)


def build(nseq=NSEQ, stages=None, final_norm=True):
    nc = bass.Bass("TRN2", target_bir_lowering=False)
    fw = FW(nc)
    pe, act, dve, pool, sp = fw.pe, fw.act, fw.dve, fw.pool, fw.sp

    uid = [0]

    def SB(name, shape, dt):
        uid[0] += 1
        return nc.sbuf_tensor(f"{name}_{uid[0]}", shape, dt)

    def din(name, shape, dt=F32):
        return nc.dram_tensor(name, list(shape), dt, kind="ExternalInput")

    x_d = din("x", [nseq, S, D])
    wgu_d = [din("ffn1_w_gu", [NL, D, 2 * DFF]), din("ffn2_w_gu", [NL, D, 2 * DFF])]
    wd_d = [din("ffn1_w_down", [NL, DFF, D]), din("ffn2_w_down", [NL, DFF, D])]
    win_d = din("w_in", [NL, D, INW])
    wout_d = din("w_out", [NL, D, D])
    gall_d = din("gall", [128, 56])
    bfb_d = din("bfb", [128, 2])
    dlb_d = din("dlb", [128, 256])
    relb_d = din("relb", [NL, 128, 4, 640])
    ident_d = din("ident", [128, 128])
    masks_d = din("masks", [128, 5, 128], BF16)
    maskc_d = din("maskc", [128, 5, 128])
    caug_d = din("caug", [4, 2, 4, S], BF16)
    visb_d = din("visb", [128, 640], BF16)
    cc_d = din("cc", [128, 16])
    y_d = nc.dram_tensor("y", [nseq, S, D], F32, kind="ExternalOutput")

    wgu_s = nc.dram_tensor("wgu_s", [NL, 2, NJ, 128, 2048], BF16, kind="Internal")
    wd_s = nc.dram_tensor("wd_s", [NL, 2, 8, 128, NJ * 128], BF16, kind="Internal")
    win_s = nc.dram_tensor("win_s", [NL, D, INW], BF16, kind="Internal")
    wout_s = nc.dram_tensor("wout_s", [NL, D, D], BF16, kind="Internal")

    pb = [nc.alloc_psum_tensor(f"pb{i}", [128, 512], F32) for i in range(8)]

    ident = nc.alloc_sbuf_tensor("sb_ident", [128, 128], F32)
    masks = nc.alloc_sbuf_tensor("sb_masks", [128, 5, 128], BF16)
    maskc = nc.alloc_sbuf_tensor("sb_maskc", [128, 5, 128], F32)
    cc = nc.alloc_sbuf_tensor("sb_cc", [128, 16], F32)
    gall = nc.alloc_sbuf_tensor("sb_gall", [128, 56], F32)
    bfb = nc.alloc_sbuf_tensor("sb_bfb", [128, 2], F32)
    lamc = nc.alloc_sbuf_tensor("lamc", [128, 8], F32)
    ones_b = masks[:, 4, :]
    mA = masks[:, 0, :]
    mD = masks[:, 1, :]
    mU = masks[:, 2, :]
    mLI = masks[:, 3, :]

    for (t, d, k) in ((ident, ident_d, "c_ident"), (masks, masks_d, "c_masks"), (maskc, maskc_d, "c_maskc"),
                      (cc, cc_d, "c_cc"), (gall, gall_d, "c_gall"), (bfb, bfb_d, "c_bfb")):
        fw.dma(sp, "cst_" + k, t[:], d.ap(), writes=[k])

    with ExitStack() as _es:
        dl = _es.enter_context(SB("dl", [128, 256], F32))
        dlt = _es.enter_context(SB("dlt", [128, 64], F32))
        dls = _es.enter_context(SB("dls", [128, 8], F32))
        fw.dma(sp, "cst_dl", dl[:], dlb_d.ap(), writes=["dl"])
        for l in range(NL):
            for pr in range(2):
                a0 = l * 128 + pr * 64
                fw.op(dve, lambda e, a0=a0, pr=pr: e.tensor_tensor(out=dlt[:, pr * 32:pr * 32 + 32], in0=dl[:, a0:a0 + 32],
                                                                  in1=dl[:, a0 + 32:a0 + 64], op=ALU.mult),
                      reads=["dl"], writes=["dlt"])
                fw.op(dve, lambda e, l=l, pr=pr: e.reduce_sum(out=dls[:, l * 2 + pr:l * 2 + pr + 1],
                                                              in_=dlt[:, pr * 32:pr * 32 + 32], axis=mybir.AxisListType.X),
                      reads=["dlt"], writes=["dls"])
        fw.op(act, lambda e: e.activation(out=dls[:, 4:8], in_=dls[:, 0:4], func=AF.Exp), reads=["dls"], writes=["dls"])
        for l in range(NL):
            fw.op(dve, lambda e, l=l: e.tensor_tensor(out=lamc[:, l:l + 1], in0=dls[:, 4 + 2 * l + 1:4 + 2 * l + 2],
                                                      in1=dls[:, 4 + 2 * l:4 + 2 * l + 1], op=ALU.subtract),
                  reads=["dls"], writes=["c_lamc"])
            fw.op(dve, lambda e, l=l: e.tensor_scalar(out=lamc[:, l:l + 1], in0=lamc[:, l:l + 1], scalar1=-_lam_init(l),
                                                      scalar2=None, op0=ALU.add),
                  reads=["c_lamc"], writes=["c_lamc"])
        fw.barrier()

    def prepass():
        with ExitStack() as _es:
            f0 = _es.enter_context(SB("ppf0", [128, 5632], F32))
            f1 = _es.enter_context(SB("ppf1", [128, 5632], F32))
            b0 = _es.enter_context(SB("ppb0", [128, 5632], BF16))
            b1 = _es.enter_context(SB("ppb1", [128, 5632], BF16))
            fbs, bbs = (f0, f1), (b0, b1)
            items = []
            for l in range(NL):
                for f in range(2):
                    for kc in range(8):
                        def st(bt, i, l=l, f=f, kc=kc):
                            dst = bass.AP(wgu_s, ((l * 2 + f) * NJ) * 128 * 2048 + kc * 256,
                                          [[2048, 128], [128 * 2048, NJ], [1, 256]])
                            src = bt[:, 0:2 * DFF].rearrange("p (j c) -> p j c", c=256)
                            fw.dma(sp, f"pps{i % 2}", dst, src, reads=[f"ppb{i % 2}"], writes=["scr"])
                        items.append((wgu_d[f][l, kc * 128:(kc + 1) * 128, :], 2 * DFF, st, "gu"))
                    for j in range(0, NJ, 2):
                        def st(bt, i, l=l, f=f, j=j):
                            dst = bass.AP(wd_s, ((l * 2 + f) * 8) * 128 * NJ * 128 + j * 128,
                                          [[NJ * 128, 128], [128 * NJ * 128, 8], [1, 256]])
                            src = bt[:, 0:2 * D].rearrange("p (o c) -> p o c", c=256)
                            fw.dma(sp, f"pps{i % 2}", dst, src, reads=[f"ppb{i % 2}"], writes=["scr"])
                        items.append((wd_d[f][l, j * 128:(j + 2) * 128, :].rearrange("(j p) d -> p j d", p=128), 2 * D, st, "d"))
                for kc in range(8):
                    def st(bt, i, l=l, kc=kc):
                        fw.dma(sp, f"pps{i % 2}", win_s[l, kc * 128:(kc + 1) * 128, :], bt[:, 0:INW], reads=[f"ppb{i % 2}"], writes=["scr"])
                    items.append((win_d[l, kc * 128:(kc + 1) * 128, :], INW, st, None))

                    def st2(bt, i, l=l, kc=kc):
                        fw.dma(sp, f"pps{i % 2}", wout_s[l, kc * 128:(kc + 1) * 128, :], bt[:, 0:D], reads=[f"ppb{i % 2}"], writes=["scr"])
                    items.append((wout_d[l, kc * 128:(kc + 1) * 128, :], D, st2, None))

            def load(i):
                src, C, _, kind = items[i]
                dstb = fbs[i % 2][:, 0:C]
                if kind == "d":
                    dstb = dstb.rearrange("p (j d) -> p j d", j=2)
                fw.dma(sp, f"ppl{i % 2}", dstb, src, writes=[f"ppf{i % 2}"])
            load(0)
            for i, (src, C, st, kind) in enumerate(items):
                if i + 1 < len(items):
                    load(i + 1)
                fb, bb = fbs[i % 2], bbs[i % 2]
                if kind == "gu":
                    ci = fb[:, 0:C].rearrange("p (h j c) -> p h j c", h=2, c=128)
                    co = bb[:, 0:C].rearrange("p (j h c) -> p h j c", h=2, c=128)
                elif kind == "d":
                    ci = fb[:, 0:C].rearrange("p (j o c) -> p j o c", j=2, c=128)
                    co = bb[:, 0:C].rearrange("p (o j c) -> p j o c", j=2, c=128)
                else:
                    ci, co = fb[:, 0:C], bb[:, 0:C]
                k = i % 3
                if k == 0:
                    fw.op(pool, lambda e, ci=ci, co=co: e.tensor_copy(out=co, in_=ci),
                          reads=[f"ppf{i % 2}"], writes=[f"ppb{i % 2}"])
                elif k == 1:
                    fw.op(dve, lambda e, ci=ci, co=co: e.tensor_copy(out=co, in_=ci),
                          reads=[f"ppf{i % 2}"], writes=[f"ppb{i % 2}"])
                else:
                    fw.op(act, lambda e, ci=ci, co=co: e.activation(out=co, in_=ci, func=AF.Copy),
                          reads=[f"ppf{i % 2}"], writes=[f"ppb{i % 2}"])
                st(bb, i)
            fw.barrier()

    prepass()

    xT = nc.alloc_sbuf_tensor("xT", [128, 8, S], F32)
    hT = nc.alloc_sbuf_tensor("hT", [128, 8, S], BF16)
    sq = nc.alloc_sbuf_tensor("sq", [128, 8, 512], BF16)
    lnv = nc.alloc_sbuf_tensor("lnv", [128, 512], F32)
    rstd = nc.alloc_sbuf_tensor("rstd", [128, 512], F32)

    def xk(fc, tt):
        return f"x{fc}_{tt}"

    def tsl(tt):
        return slice(tt * 512, (tt + 1) * 512)

    def norm_stats(tt, bank):
        fw.op(act, lambda e: e.activation(out=sq[:], in_=xT[:, :, tsl(tt)], func=AF.Square),
              reads=[xk(fc, tt) for fc in range(8)], writes=["sq"])
        fw.op(pe, [lambda e, fc=fc: e.matmul(pb[bank][:], lhsT=ones_b, rhs=sq[:, fc, :], start=(fc == 0), stop=(fc == 7))
                   for fc in range(8)], reads=["sq", "c_masks"], writes=[f"pb{bank}"])
        fw.op(act, lambda e: e.activation(out=lnv[:], in_=pb[bank][:], func=AF.Ln, bias=cc[:, 0:1], scale=1.0 / D),
              reads=[f"pb{bank}", "c_cc"], writes=["lnv"])
        fw.op(act, lambda e: e.activation(out=rstd[:], in_=lnv[:], func=AF.Exp, scale=-0.5),
              reads=["lnv"], writes=["rstd"])

    def norm_to_h(gidx, tt, hcol0, bank=6):
        norm_stats(tt, bank)
        for fc in range(8):
            fw.op(dve, lambda e, fc=fc: e.scalar_tensor_tensor(
                out=hT[:, fc, hcol0:hcol0 + 512], in0=xT[:, fc, tsl(tt)], scalar=gall[:, gidx * 8 + fc:gidx * 8 + fc + 1],
                in1=rstd[:], op0=ALU.mult, op1=ALU.mult),
                reads=[xk(fc, tt), "rstd", "c_gall"], writes=[f"h{hcol0 // 512}"])

    def load_x(b):
        with ExitStack() as _es:
            xt0 = _es.enter_context(SB("xtok0", [128, D], F32))
            xt1 = _es.enter_context(SB("xtok1", [128, D], F32))
            xts = (xt0, xt1)
            for tc in range(16):
                xt = xts[tc % 2]
                fw.dma(sp, f"xl{tc % 2}", xt[:], x_d[b, tc * 128:(tc + 1) * 128, :], writes=[f"xtok{tc % 2}"])
                for g in range(2):
                    bank = 6 + g
                    fw.op(pe, [lambda e, q=q, g=g, xt=xt, bank=bank: e.transpose(
                        pb[bank][:, q * 128:(q + 1) * 128], xt[:, (g * 4 + q) * 128:(g * 4 + q + 1) * 128], ident[:])
                        for q in range(4)], reads=[f"xtok{tc % 2}", "c_ident"], writes=[f"pb{bank}"])
                    src = pb[bank][:].rearrange("p (q c) -> p q c", c=128)
                    dst = xT[:, g * 4:(g + 1) * 4, tc * 128:(tc + 1) * 128]
                    wk = [xk(fc, tc // 4) for fc in range(g * 4, g * 4 + 4)]
                    if g == 0:
                        fw.op(act, lambda e, src=src, dst=dst: e.activation(out=dst, in_=src, func=AF.Copy),
                              reads=[f"pb{bank}"], writes=wk)
                    else:
                        fw.op(dve, lambda e, src=src, dst=dst: e.tensor_copy(out=dst, in_=src),
                              reads=[f"pb{bank}"], writes=wk)
            fw.barrier()

    def store_x(b, do_norm):
        with ExitStack() as _es:
            yT = _es.enter_context(SB("yT", [128, 8, 512], F32))
            yt0 = _es.enter_context(SB("ytok0", [128, D], F32))
            yt1 = _es.enter_context(SB("ytok1", [128, D], F32))
            yts = (yt0, yt1)
            cnt = 0
            for tt in range(4):
                if do_norm:
                    norm_stats(tt, 5)
                    for fc in range(8):
                        fw.op(dve, lambda e, fc=fc: e.scalar_tensor_tensor(
                            out=yT[:, fc, :], in0=xT[:, fc, tsl(tt)], scalar=gall[:, 48 + fc:48 + fc + 1],
                            in1=rstd[:], op0=ALU.mult, op1=ALU.mult),
                            reads=[xk(fc, tt), "rstd", "c_gall"], writes=["yT"])
                else:
                    fw.op(dve, lambda e: e.tensor_copy(out=yT[:], in_=xT[:, :, tsl(tt)]),
                          reads=[xk(fc, tt) for fc in range(8)], writes=["yT"])
                for tcl in range(4):
                    yt = yts[cnt % 2]
                    for g in range(2):
                        bank = 6 + g
                        fw.op(pe, [lambda e, q=q, g=g, bank=bank, tcl=tcl: e.transpose(
                            pb[bank][:, q * 128:(q + 1) * 128], yT[:, g * 4 + q, tcl * 128:(tcl + 1) * 128], ident[:])
                            for q in range(4)], reads=["yT", "c_ident"], writes=[f"pb{bank}"])
                        if g == 0:
                            fw.op(act, lambda e, yt=yt, bank=bank: e.activation(out=yt[:, 0:512], in_=pb[bank][:], func=AF.Copy),
                                  reads=[f"pb{bank}"], writes=[f"ytok{cnt % 2}a"])
                        else:
                            fw.op(dve, lambda e, yt=yt, bank=bank: e.tensor_copy(out=yt[:, 512:1024], in_=pb[bank][:]),
                                  reads=[f"pb{bank}"], writes=[f"ytok{cnt % 2}b"])
                    r0 = tt * 512 + tcl * 128
                    fw.dma(sp, f"yst{cnt % 2}", y_d[b, r0:r0 + 128, :], yt[:],
                           reads=[f"ytok{cnt % 2}a", f"ytok{cnt % 2}b"], writes=["y_out"])
                    cnt += 1
            fw.barrier()

    def ffn(l, f):
        gidx = l * 3 + (0 if f == 0 else 2)
        with ExitStack() as _es:
            actT = _es.enter_context(SB("actT", [128, NJ, 1024], BF16))
            wg0 = _es.enter_context(SB("wgb0", [128, 8, 256], BF16))
            wg1 = _es.enter_context(SB("wgb1", [128, 8, 256], BF16))
            wg2 = _es.enter_context(SB("wgb2", [128, 8, 256], BF16))
            wdb0 = _es.enter_context(SB("wdb0", [128, NJ, 128], BF16))
            wdb1 = _es.enter_context(SB("wdb1", [128, NJ, 128], BF16))
            sg0 = _es.enter_context(SB("sg0", [128, 512], F32))
            sg1 = _es.enter_context(SB("sg1", [128, 512], F32))
            wgs = (wg0, wg1, wg2)
            wds = (wdb0, wdb1)
            sgs = (sg0, sg1)
            st = {"g": 0, "d": 0, "wg": 0, "wd": 0}

            def do_norm(half):
                for t2 in range(2):
                    norm_to_h(gidx, half * 2 + t2, half * 1024 + t2 * 512)

            def do_gu(half):
                for j in range(NJ):
                    wi = st["wg"] % 3
                    st["wg"] += 1
                    wg = wgs[wi]
                    fw.dma(sp, f"wg{wi}", wg[:].rearrange("p a b -> p (a b)"), wgu_s[l, f, j], reads=["scr"], writes=[f"wg{wi}"])
                    for t2 in range(2):
                        c = st["g"] % 2
                        st["g"] += 1
                        gb, ub = 2 * c, 2 * c + 1
                        hc = half * 1024 + t2 * 512
                        hk = f"h{hc // 512}"
                        fw.op(pe, [lambda e, kc=kc, wg=wg, gb=gb, hc=hc: e.matmul(
                            pb[gb][:], lhsT=wg[:, kc, 0:128], rhs=hT[:, kc, hc:hc + 512], start=(kc == 0), stop=(kc == 7))
                            for kc in range(8)], reads=[f"wg{wi}", hk], writes=[f"pb{gb}"])
                        fw.op(pe, [lambda e, kc=kc, wg=wg, ub=ub, hc=hc: e.matmul(
                            pb[ub][:], lhsT=wg[:, kc, 128:256], rhs=hT[:, kc, hc:hc + 512], start=(kc == 0), stop=(kc == 7))
                            for kc in range(8)], reads=[f"wg{wi}", hk], writes=[f"pb{ub}"])
                        sg = sgs[c]
                        fw.op(act, lambda e, sg=sg, gb=gb: e.activation(out=sg[:], in_=pb[gb][:], func=AF.Silu),
                              reads=[f"pb{gb}"], writes=[f"sg{c}"])
                        fw.op(dve, lambda e, sg=sg, ub=ub, j=j, t2=t2: e.tensor_tensor(
                            out=actT[:, j, t2 * 512:(t2 + 1) * 512], in0=pb[ub][:], in1=sg[:], op=ALU.mult),
                            reads=[f"pb{ub}", f"sg{c}"], writes=[f"a{j}_{t2}"])

            def do_down(half):
                for oc in range(8):
                    wi = st["wd"] % 2
                    st["wd"] += 1
                    wd = wds[wi]
                    fw.dma(sp, f"wd{wi}", wd[:].rearrange("p a b -> p (a b)"), wd_s[l, f, oc], reads=["scr"], writes=[f"wd{wi}"])
                    for t2 in range(2):
                        bank = 4 + st["d"] % 2
                        st["d"] += 1
                        tt = half * 2 + t2
                        fw.op(pe, [lambda e, j=j, wd=wd, bank=bank, t2=t2: e.matmul(
                            pb[bank][:], lhsT=wd[:, j, :], rhs=actT[:, j, t2 * 512:(t2 + 1) * 512], start=(j == 0), stop=(j == NJ - 1))
                            for j in range(NJ)], reads=[f"wd{wi}"] + [f"a{j}_{t2}" for j in range(NJ)], writes=[f"pb{bank}"])
                        fw.op(dve, lambda e, bank=bank, oc=oc, tt=tt: e.scalar_tensor_tensor(
                            out=xT[:, oc, tsl(tt)], in0=pb[bank][:], scalar=0.5, in1=xT[:, oc, tsl(tt)],
                            op0=ALU.mult, op1=ALU.add), reads=[f"pb{bank}", xk(oc, tt)], writes=[xk(oc, tt)])

            do_norm(0)
            do_gu(0)
            do_norm(1)
            do_down(0)
            do_gu(1)
            do_down(1)
            fw.barrier()

    def mixer(l):
        lam0 = _lam_init(l)
        with ExitStack() as _es:
            wbuf = _es.enter_context(SB("wbuf", [128, 8, 772], BF16))
            t0 = _es.enter_context(SB("qk0", [128, S], BF16))
            t1 = _es.enter_context(SB("qk1", [128, S], BF16))
            t2_ = _es.enter_context(SB("qk2", [128, S], BF16))
            t3 = _es.enter_context(SB("qk3", [128, S], BF16))
            t4 = _es.enter_context(SB("qk4", [128, S], BF16))
            t5 = _es.enter_context(SB("qk5", [128, S], BF16))
            t6 = _es.enter_context(SB("qk6", [128, S], BF16))
            t7 = _es.enter_context(SB("qk7", [128, S], BF16))
            Vt = _es.enter_context(SB("Vt", [128, 16, 4, 128], BF16))
            oT0 = _es.enter_context(SB("oT0", [128, S], BF16))
            oT1 = _es.enter_context(SB("oT1", [128, S], BF16))
            Pts = [_es.enter_context(SB(f"Pt{i}", [128, 512], BF16)) for i in range(6)]
            rb0 = _es.enter_context(SB("rb0", [128, 512], F32))
            rb1 = _es.enter_context(SB("rb1", [128, 512], F32))
            qk = (t0, t1, t2_, t3, t4, t5, t6, t7)
            oTs = (oT0, oT1)
            rbs = (rb0, rb1)
            ctr = {"s": 0, "p": 0, "o": 0, "pj": 0, "v": 0, "w": 0, "r": 0}

            def next_s():
                ctr["s"] += 1
                return ctr["s"] % 3

            def next_p():
                ctr["p"] += 1
                return ctr["p"] % 3

            def load_win(m_):
                base_ = (0, 772, 1540, 2308)[m_]
                ncols_ = 772 if m_ == 0 else 768
                fw.dma(sp, "wb", wbuf[:, :, 0:ncols_],
                       bass.AP(win_s, l * D * INW + base_, [[INW, 128], [128 * INW, 8], [1, ncols_]]),
                       reads=["scr"], writes=["wbuf"])

            load_win(MIXERS[0])
            for tt in range(4):
                norm_to_h(l * 3 + 1, tt, tt * 512)
            wo = bass.AP(sq, 0, [[8 * 512, 128], [1024, 2], [1, 1024]])
            for mi, m in enumerate(MIXERS):
                qscale = (0.125, 0.125, 32 ** -0.5, 0.125)[m]
                fw.dma(sp, "wo", wo, bass.AP(wout_s, l * D * D + m * 256 * D, [[D, 128], [128 * D, 2], [1, D]]),
                       reads=["scr"], writes=["sq"])
                nxt = MIXERS[mi + 1] if mi + 1 < len(MIXERS) else None
                fw.op(pool, lambda e: e.memset(Vt[:, :, :, 64:128], 1.0), writes=["Vt"])
                def vproj():
                    for tc in range(16):
                        bank = 3 + ctr["v"] % 2
                        ctr["v"] += 1
                        fw.op(pe, [lambda e, kc=kc, tc=tc, bank=bank: e.matmul(
                            pb[bank][:, 0:256], lhsT=hT[:, kc, tc * 128:(tc + 1) * 128], rhs=wbuf[:, kc, 512:768],
                            start=(kc == 0), stop=(kc == 7)) for kc in range(8)],
                            reads=["wbuf", f"h{tc // 4}"], writes=[f"pb{bank}"])
                        src = pb[bank][:, 0:256].rearrange("p (h c) -> p h c", c=64)
                        if tc % 2 == 0:
                            fw.op(act, lambda e, src=src, tc=tc: e.activation(out=Vt[:, tc, :, 0:64], in_=src, func=AF.Copy),
                                  reads=[f"pb{bank}"], writes=["Vt"])
                        else:
                            fw.op(dve, lambda e, src=src, tc=tc: e.tensor_copy(out=Vt[:, tc, :, 0:64], in_=src),
                                  reads=[f"pb{bank}"], writes=["Vt"])
                    if nxt is not None:
                        load_win(nxt)
                def qkproj(act_only=False):
                    for which in range(2):
                        for pair in range(2):
                            c0 = which * 256 + pair * 128
                            for tt in range(4):
                                bank = ctr["pj"] % 2
                                ctr["pj"] += 1
                                fw.op(pe, [lambda e, kc=kc, c0=c0, tt=tt, bank=bank: e.matmul(
                                    pb[bank][:], lhsT=wbuf[:, kc, c0:c0 + 128], rhs=hT[:, kc, tsl(tt)],
                                    start=(kc == 0), stop=(kc == 7)) for kc in range(8)],
                                    reads=["wbuf", f"h{tt}"], writes=[f"pb{bank}"])
                                sc = qscale if which == 0 else 1.0
                                if m != 2:
                                    pieces = [(0, 64, qk[which * 4 + 2 * pair], 0), (64, 64, qk[which * 4 + 2 * pair + 1], 0)]
                                else:
                                    ta, tb = qk[which * 4 + 2 * pair], qk[which * 4 + 2 * pair + 1]
                                    pieces = [(0, 32, ta, 0), (32, 32, ta, 64), (64, 32, tb, 0), (96, 32, tb, 64)]
                                for pi, (r0, nr, dst, d0) in enumerate(pieces):
                                    hidx = qk.index(dst)
                                    if pi % 2 == 0 or act_only:
                                        fw.op(act, lambda e, r0=r0, nr=nr, dst=dst, d0=d0, bank=bank, tt=tt, sc=sc: e.activation(
                                            out=dst[d0:d0 + nr, tsl(tt)], in_=pb[bank][r0:r0 + nr, :], func=AF.Copy, scale=sc),
                                            reads=[f"pb{bank}"], writes=[f"qk{hidx}"])
                                    else:
                                        fw.op(dve, lambda e, r0=r0, nr=nr, dst=dst, d0=d0, bank=bank, tt=tt, sc=sc: e.tensor_scalar(
                                            out=dst[d0:d0 + nr, tsl(tt)], in0=pb[bank][r0:r0 + nr, :], scalar1=sc, scalar2=None,
                                            op0=ALU.mult), reads=[f"pb{bank}"], writes=[f"qk{hidx}"])

                def finalize_softmax(h, qt, ob):
                    ri = ctr["r"] % 2
                    ctr["r"] += 1
                    rb = rbs[ri]
                    fw.op(act, lambda e: e.activation(out=rb[64:128, :], in_=pb[ob][64:128, :], func=AF.Ln),
                          reads=[f"pb{ob}"], writes=[f"rb{ri}"])
                    fw.op(act, lambda e: e.activation(out=rb[64:128, :], in_=rb[64:128, :], func=AF.Exp, scale=-1.0),
                          reads=[f"rb{ri}"], writes=[f"rb{ri}"])
                    dst = oTs[h // 2][(h % 2) * 64:(h % 2) * 64 + 64, tsl(qt)]
                    fw.op(dve, lambda e: e.tensor_tensor(out=dst, in0=pb[ob][0:64, :], in1=rb[64:128, :], op=ALU.mult),
                          reads=[f"pb{ob}", f"rb{ri}"], writes=[f"oT{h // 2}"])

                def run_pipe(steps, LA=2, mid=None):
                    n = len(steps)
                    for i in range(n + LA):
                        if i == LA and mid is not None:
                            mid()
                        if i < n:
                            steps[i][0]()
                        if i >= LA:
                            steps[i - LA][1]()

                def pv_back(h, j, n, c0, ob, first, last, pi, fin):
                    Pt = Pts[pi]

                    def back():
                        fw.op(pe, lambda e: e.matmul(
                            pb[ob][:, c0:c0 + n], lhsT=Vt[:, j, h, :], rhs=Pt[:, 0:n], start=first, stop=last,
                            skip_group_check=True), reads=["Vt", f"Pt{pi}"], writes=[f"pb{ob}"])
                        if last and fin is not None:
                            fin()
                    return back

                def attn_A(dtmps):
                    steps = []
                    gi = 0
                    for h in range(4):
                        qa, ka = qk[h], qk[4 + h]
                        for qt in range(4):
                            ob = 3 + ctr["o"] % 2
                            ctr["o"] += 1
                            jmax = 4 * qt + 3
                            for j in range(jmax + 1):
                                q0 = max(j, 4 * qt) * 128
                                n = (4 * qt + 4) * 128 - q0
                                c0 = q0 - qt * 512
                                sb = (0, 1, 2, 5, 6)[gi % 5]
                                pi = gi % 6
                                di = gi % 2
                                gi += 1
                                diag = j >= 4 * qt

                                def front(h=h, qa=qa, ka=ka, j=j, q0=q0, n=n, sb=sb, pi=pi, di=di, diag=diag):
                                    Pt = Pts[pi]
                                    fw.op(pe, lambda e: e.matmul(
                                        pb[sb][:, 0:n], lhsT=ka[0:70, j * 128:(j + 1) * 128], rhs=qa[0:70, q0:q0 + n],
                                        start=True, stop=True),
                                        reads=[f"qk{h}", f"qk{4 + h}"] + [f"qa{h}_{t_}" for t_ in range(4)] + [f"qa{4 + h}_{t_}" for t_ in range(4)],
                                        writes=[f"pb{sb}"])
                                    if diag:
                                        dtmp = dtmps[di]
                                        fw.op(dve, lambda e: e.tensor_tensor(out=dtmp[:, 0:128], in0=pb[sb][:, 0:128], in1=maskc[:, 4, :], op=ALU.add),
                                              reads=[f"pb{sb}", "c_maskc"], writes=[f"dtmp{di}"])
                                        fw.op(act, lambda e: e.activation(out=Pt[:, 0:128], in_=dtmp[:, 0:128], func=AF.Exp),
                                              reads=[f"dtmp{di}"], writes=[f"Pt{pi}"])
                                        if n > 128:
                                            fw.op(act, lambda e: e.activation(out=Pt[:, 128:n], in_=pb[sb][:, 128:n], func=AF.Exp),
                                                  reads=[f"pb{sb}"], writes=[f"Pt{pi}"])
                                    else:
                                        fw.op(act, lambda e: e.activation(out=Pt[:, 0:n], in_=pb[sb][:, 0:n], func=AF.Exp),
                                              reads=[f"pb{sb}"], writes=[f"Pt{pi}"])
                                fin = (lambda h=h, qt=qt, ob=ob: finalize_softmax(h, qt, ob))
                                steps.append((front, pv_back(h, j, n, c0, ob, j == 0, j == jmax, pi, fin)))
                    run_pipe(steps, LA=5, mid=vproj)

                def attn_B(EB, tbs):
                    steps = []
                    gi = 0
                    for h in range(4):
                        qa, ka = qk[h], qk[4 + h]
                        for qt in range(4):
                            ob = 3 + ctr["o"] % 2
                            ctr["o"] += 1
                            js = [4 * qt] + [j for j in range(max(0, 4 * qt - 4), 4 * qt + 4) if j != 4 * qt]
                            for idx, j in enumerate(js):
                                i0 = max(j, 4 * qt)
                                i1 = min(j + 4, 4 * qt + 3)
                                q0 = i0 * 128
                                n = (i1 - i0 + 1) * 128
                                eoff = (i0 - j) * 128
                                c0 = q0 - qt * 512
                                sb = (0, 1, 2, 5, 6)[gi % 5]
                                pi = gi % 6
                                ti = gi % 2
                                gi += 1

                                def front(h=h, qa=qa, ka=ka, j=j, q0=q0, n=n, sb=sb, pi=pi, ti=ti, eoff=eoff):
                                    Pt = Pts[pi]
                                    tb = tbs[ti]
                                    fw.op(pe, lambda e: e.matmul(
                                        pb[sb][:, 0:n], lhsT=ka[:, j * 128:(j + 1) * 128], rhs=qa[:, q0:q0 + n],
                                        start=True, stop=True), reads=[f"qk{h}", f"qk{4 + h}"], writes=[f"pb{sb}"])
                                    fw.op(act, lambda e: e.activation(out=tb[:, 0:n], in_=pb[sb][:, 0:n], func=AF.Exp),
                                          reads=[f"pb{sb}"], writes=[f"tb{ti}"])
                                    fw.op(dve, lambda e: e.tensor_tensor(
                                        out=Pt[:, 0:n], in0=tb[:, 0:n], in1=EB[:, h, eoff:eoff + n], op=ALU.mult),
                                        reads=[f"tb{ti}", "EB"], writes=[f"Pt{pi}"])
                                fin = (lambda h=h, qt=qt, ob=ob: finalize_softmax(h, qt, ob))
                                steps.append((front, pv_back(h, j, n, c0, ob, idx == 0, idx == len(js) - 1, pi, fin)))
                    run_pipe(steps, LA=5, mid=vproj)

                def attn_C(c1s, c2s, csq, cl):
                    steps = []
                    gi = 0
                    tile_i = 0
                    H = slice(0, 64)
                    for h in range(4):
                        qa, ka = qk[h], qk[4 + h]
                        for qt in range(4):
                            obs = (3, 4) if tile_i % 2 == 0 else (5, 6)
                            c1, c2 = c1s[tile_i % 2], c2s[tile_i % 2]
                            ck = tile_i % 2
                            tile_i += 1
                            jmax = 4 * qt + 3

                            def fin(h=h, qt=qt, obs=obs, c1=c1, c2=c2, ck=ck):
                                o1, o2 = obs
                                for (ob, rb, rk) in ((o1, rb0, "rb0"), (o2, rb1, "rb1")):
                                    fw.op(act, lambda e, ob=ob, rb=rb: e.activation(out=rb[64:128, :], in_=pb[ob][64:128, :], func=AF.Ln),
                                          reads=[f"pb{ob}"], writes=[rk])
                                    fw.op(act, lambda e, rb=rb: e.activation(out=rb[64:128, :], in_=rb[64:128, :], func=AF.Exp, scale=-1.0),
                                          reads=[rk], writes=[rk])
                                fw.op(dve, lambda e: e.tensor_tensor(out=c1[H, :], in0=pb[o1][H, :], in1=rb0[64:128, :], op=ALU.mult),
                                      reads=[f"pb{o1}", "rb0"], writes=[f"c1{ck}"])
                                fw.op(dve, lambda e: e.tensor_tensor(out=c2[H, :], in0=pb[o2][H, :], in1=rb1[64:128, :], op=ALU.mult),
                                      reads=[f"pb{o2}", "rb1"], writes=[f"c2{ck}"])
                                fw.op(dve, lambda e: e.scalar_tensor_tensor(out=c1[H, :], in0=c2[H, :], scalar=lamc[H, l:l + 1], in1=c1[H, :],
                                                                            op0=ALU.mult, op1=ALU.add),
                                      reads=[f"c1{ck}", f"c2{ck}", "c_lamc"], writes=[f"c1{ck}"])
                                fw.op(act, lambda e: e.activation(out=csq[H, :], in_=c1[H, :], func=AF.Square), reads=[f"c1{ck}"], writes=["csq"])
                                fw.op(pe, lambda e: e.matmul(pb[7][H, :], lhsT=masks[H, 4, 0:64], rhs=csq[H, :], start=True, stop=True),
                                      reads=["csq", "c_masks"], writes=["pb7"])
                                fw.op(act, lambda e: e.activation(out=cl[H, :], in_=pb[7][H, :], func=AF.Ln, bias=cc[H, 0:1], scale=1.0 / 64),
                                      reads=["pb7", "c_cc"], writes=["cl"])
                                fw.op(act, lambda e: e.activation(out=cl[H, :], in_=cl[H, :], func=AF.Exp, scale=-0.5), reads=["cl"], writes=["cl"])
                                dst = oTs[h // 2][(h % 2) * 64:(h % 2) * 64 + 64, tsl(qt)]
                                fw.op(dve, lambda e: e.scalar_tensor_tensor(out=dst, in0=c1[H, :], scalar=1.0 - lam0, in1=cl[H, :],
                                                                            op0=ALU.mult, op1=ALU.mult),
                                      reads=[f"c1{ck}", "cl"], writes=[f"oT{h // 2}"])

                            for j in range(jmax + 1):
                                q0 = max(j, 4 * qt) * 128
                                n = (4 * qt + 4) * 128 - q0
                                c0 = q0 - qt * 512
                                diag = j >= 4 * qt
                                sbs = ((0, 1), (2, 7))[gi % 2]
                                pis = ((2 * gi) % 6, (2 * gi + 1) % 6)
                                gi += 1

                                def front(h=h, qa=qa, ka=ka, j=j, q0=q0, n=n, sbs=sbs, pis=pis, diag=diag):
                                    fw.op(pe, [lambda e, mp=mp: e.matmul(
                                        pb[sbs[mp]][:, 0:n], lhsT=ka[mp * 64:mp * 64 + 36, j * 128:(j + 1) * 128],
                                        rhs=qa[mp * 64:mp * 64 + 36, q0:q0 + n], start=True, stop=True) for mp in range(2)],
                                        reads=[f"qk{h}", f"qk{4 + h}"], writes=[f"pb{sbs[0]}", f"pb{sbs[1]}"])
                                    for mp in range(2):
                                        Pt = Pts[pis[mp]]
                                        fw.op(act, lambda e, Pt=Pt, mp=mp: e.activation(out=Pt[:, 0:n], in_=pb[sbs[mp]][:, 0:n], func=AF.Exp),
                                              reads=[f"pb{sbs[mp]}"], writes=[f"Pt{pis[mp]}"])
                                        if diag:
                                            fw.op(dve, lambda e, Pt=Pt: e.tensor_tensor(out=Pt[:, 0:128], in0=Pt[:, 0:128], in1=maskc[:, h, :], op=ALU.mult),
                                                  reads=[f"Pt{pis[mp]}", "c_maskc"], writes=[f"Pt{pis[mp]}"])

                                def back(h=h, j=j, n=n, c0=c0, obs=obs, pis=pis, jmax=jmax, fin=fin):
                                    for mp in range(2):
                                        Pt = Pts[pis[mp]]
                                        fw.op(pe, lambda e, Pt=Pt, mp=mp: e.matmul(
                                            pb[obs[mp]][:, c0:c0 + n], lhsT=Vt[:, j, h, :], rhs=Pt[:, 0:n], start=(j == 0), stop=(j == jmax),
                                            skip_group_check=True), reads=["Vt", f"Pt{pis[mp]}"], writes=[f"pb{obs[mp]}"])
                                    if j == jmax:
                                        fin()
                                steps.append((front, back))
                    run_pipe(steps, LA=2, mid=vproj)

                def attn_D(e32s, l32s, lbs, t1s):
                    st = []
                    tile_i = 0
                    for h in range(4):
                        for qt in range(4):
                            ob = 3 + tile_i % 2
                            tile_i += 1
                            jmax = 4 * qt + 3
                            for k, j in enumerate(range(jmax, -1, -1)):
                                q0 = max(j, 4 * qt) * 128
                                st.append(dict(h=h, qt=qt, ob=ob, k=k, K=jmax + 1, j=j, q0=q0, n=(4 * qt + 4) * 128 - q0,
                                               c0=q0 - qt * 512, diag=(j >= 4 * qt)))
                    G = len(st)
                    SBK = (0, 1, 2, 7)

                    def F1(g):
                        s = st[g]
                        h, j, q0, n = s["h"], s["j"], s["q0"], s["n"]
                        sb = SBK[g % 4]
                        e32, l32, lb = e32s[g % 2], l32s[g % 3], lbs[g % 3]
                        qa, ka = qk[h], qk[4 + h]
                        fw.op(pe, lambda e: e.matmul(
                            pb[sb][:, 0:n], lhsT=ka[:, j * 128:(j + 1) * 128], rhs=qa[:, q0:q0 + n],
                            start=True, stop=True), reads=[f"qk{h}", f"qk{4 + h}"], writes=[f"pb{sb}"])
                        fw.op(act, lambda e: e.activation(out=e32[:, 0:n], in_=pb[sb][:, 0:n], func=AF.Exp),
                              reads=[f"pb{sb}"], writes=[f"e32{g % 2}"])
                        fw.op(act, lambda e: e.activation(out=l32[:, 0:n], in_=e32[:, 0:n], func=AF.Ln, bias=cc[:, 1:2], scale=1.0),
                              reads=[f"e32{g % 2}", "c_cc"], writes=[f"l32{g % 3}"])

                    def F1b(g):
                        s = st[g]
                        n = s["n"]
                        l32, lb = l32s[g % 3], lbs[g % 3]
                        fw.op(dve, lambda e: e.tensor_copy(out=lb[:, 0:n], in_=l32[:, 0:n]),
                              reads=[f"l32{g % 3}"], writes=[f"lb{g % 3}"])
                        if s["diag"]:
                            fw.op(pool, lambda e: e.tensor_tensor(out=lb[:, 0:128], in0=lb[:, 0:128], in1=mD, op=ALU.mult),
                                  reads=[f"lb{g % 3}", "c_masks"], writes=[f"lb{g % 3}"])

                    def F3(g):
                        s = st[g]
                        n, c0, k = s["n"], s["c0"], s["k"]
                        sb = SBK[g % 4]
                        l32, t1, Pt = l32s[g % 3], t1s[g % 2], Pts[g % 3]
                        rbk = 5 + k % 2
                        fw.op(dve, lambda e: e.tensor_tensor(out=t1[:, 0:n], in0=pb[sb][:, 0:n], in1=l32[:, 0:n], op=ALU.subtract),
                              reads=[f"pb{sb}", f"l32{g % 3}"], writes=[f"t1{g % 2}"])
                        fw.op(dve, lambda e: e.tensor_tensor(out=t1[:, 0:n], in0=t1[:, 0:n], in1=pb[rbk][:, c0:c0 + n], op=ALU.subtract),
                              reads=[f"t1{g % 2}", f"pb{rbk}"], writes=[f"t1{g % 2}"])
                        fw.op(act, lambda e: e.activation(out=Pt[:, 0:n], in_=t1[:, 0:n], func=AF.Exp),
                              reads=[f"t1{g % 2}"], writes=[f"Pt{g % 3}"])
                        if s["diag"]:
                            fw.op(pool, lambda e: e.tensor_tensor(out=Pt[:, 0:128], in0=Pt[:, 0:128], in1=mD, op=ALU.mult),
                                  reads=[f"Pt{g % 3}", "c_masks"], writes=[f"Pt{g % 3}"])

                    def PE_U(g):
                        s = st[g]
                        n, c0, k = s["n"], s["c0"], s["k"]
                        lb = lbs[g % 3]
                        b, bo = 5 + k % 2, 5 + (k + 1) % 2
                        fw.op(pe, lambda e: e.matmul(pb[b][:, c0:c0 + n], lhsT=mU, rhs=lb[:, 0:n], start=(k == 0), stop=False,
                                                     skip_group_check=True),
                              reads=[f"lb{g % 3}", "c_masks"], writes=[f"pb{b}"])
                        if k == 0:
                            fw.op(pe, lambda e: e.matmul(pb[bo][:, c0:c0 + n], lhsT=ones_b, rhs=lb[:, 0:n], start=True, stop=False,
                                                         skip_group_check=True),
                                  reads=[f"lb{g % 3}", "c_masks"], writes=[f"pb{bo}"])

                    def PE_fix(g):
                        s = st[g]
                        k, K = s["k"], s["K"]
                        if k > K - 3:
                            return
                        s1 = st[g + 1]
                        b = 5 + k % 2
                        lb0, lb1 = lbs[g % 3], lbs[(g + 1) % 3]
                        fw.op(pe, [lambda e: e.matmul(pb[b][:, s["c0"]:s["c0"] + s["n"]], lhsT=mLI, rhs=lb0[:, 0:s["n"]], start=False, stop=False,
                                                      skip_group_check=True),
                                   lambda e: e.matmul(pb[b][:, s1["c0"]:s1["c0"] + s1["n"]], lhsT=ones_b, rhs=lb1[:, 0:s1["n"]], start=False, stop=False,
                                                      skip_group_check=True)],
                              reads=[f"lb{g % 3}", f"lb{(g + 1) % 3}", "c_masks"], writes=[f"pb{b}"])

                    def PE_PV(g):
                        s = st[g]
                        h, qt, ob, k, K, j, n, c0 = s["h"], s["qt"], s["ob"], s["k"], s["K"], s["j"], s["n"], s["c0"]
                        Pt = Pts[g % 3]
                        fw.op(pe, lambda e: e.matmul(
                            pb[ob][:, c0:c0 + n], lhsT=Vt[:, j, h, :], rhs=Pt[:, 0:n], start=(k == 0), stop=(k == K - 1),
                            skip_group_check=True), reads=["Vt", f"Pt{g % 3}"], writes=[f"pb{ob}"])
                        if k == K - 1:
                            dst = oTs[h // 2][(h % 2) * 64:(h % 2) * 64 + 64, tsl(qt)]
                            fw.op(act, lambda e: e.activation(out=dst, in_=pb[ob][0:64, :], func=AF.Copy),
                                  reads=[f"pb{ob}"], writes=[f"oT{h // 2}"])

                    for it in range(-2, G + 1):
                        if it == 0:
                            vproj()
                        if 0 <= it - 1 < G:
                            PE_fix(it - 1)
                        if 0 <= it + 2 < G:
                            F1(it + 2)
                        if 0 <= it < G:
                            F3(it)
                        if 0 <= it + 2 < G:
                            F1b(it + 2)
                        if 0 <= it + 1 < G:
                            PE_U(it + 1)
                        if 0 <= it - 1 < G:
                            PE_PV(it - 1)

                if m == 0:
                    def _attn():
                        with ExitStack() as _es:
                            FA = _es.enter_context(SB("FA", [128, 512], F32))
                            ACC = _es.enter_context(SB("ACC", [128, 512], F32))
                            RR = _es.enter_context(SB("RR", [128, 512], F32))
                            AQ = _es.enter_context(SB("AQ", [128, 512], F32))
                            HB = _es.enter_context(SB("HB", [128, 512], BF16))
                            MB = _es.enter_context(SB("MB", [128, 512], BF16))
                            LB = _es.enter_context(SB("LB", [128, 512], BF16))
                            carry = _es.enter_context(SB("carry", [128, 2], F32))
                            dtmps = [_es.enter_context(SB(f"dtmp{i}", [128, 128], F32)) for i in range(2)]
                            wfa = _es.enter_context(SB("wfa", [128, 8, 4, 6], BF16))
                            STs = [[_es.enter_context(SB(f"ST{w}{i}", [128, 512], BF16)) for i in range(2)] for w in range(2)]
                            fw.op(dve, lambda e: e.tensor_copy(
                                out=wfa[:], in_=bass.AP(wbuf, 768, [[8 * 772, 128], [772, 8], [1, 4], [0, 6]])),
                                reads=["wbuf"], writes=["wfa"])
                            R = slice(0, 24)
                            for tt in range(4):
                                bank = (5, 6, 7, 4)[tt]
                                fw.op(pe, [lambda e, kc=kc, tt=tt, bank=bank: e.matmul(
                                    pb[bank][R, :], lhsT=wfa[:, kc, :, :].rearrange("p h r -> p (h r)"), rhs=hT[:, kc, tsl(tt)],
                                    start=(kc == 0), stop=(kc == 7)) for kc in range(8)],
                                    reads=["wfa", f"h{tt}"], writes=[f"pb{bank}"])
                                fw.op(dve, lambda e, bank=bank: e.tensor_scalar(
                                    out=FA[R, :], in0=pb[bank][R, :], scalar1=bfb[R, l:l + 1], scalar2=None,
                                    op0=ALU.add), reads=[f"pb{bank}", "c_bfb"], writes=["FA"])
                                fw.op(act, lambda e: e.activation(out=FA[R, :], in_=FA[R, :], func=AF.Exp, scale=-1.0),
                                      reads=["FA"], writes=["FA"])
                                fw.op(act, lambda e: e.activation(out=FA[R, :], in_=FA[R, :], func=AF.Ln, bias=cc[R, 1:2], scale=1.0),
                                      reads=["FA", "c_cc"], writes=["FA"])
                                ini = 0.0 if tt == 0 else carry[R, 0:1]
                                fw.op(dve, lambda e, ini=ini: e.tensor_tensor_scan(out=ACC[R, :], data0=FA[R, :], data1=FA[R, :], initial=ini,
                                                                                   op0=ALU.add, op1=ALU.max),
                                      reads=["FA", "carry"], writes=["ACC"])
                                fw.op(dve, lambda e: e.tensor_copy(out=carry[R, 0:1], in_=ACC[R, 511:512]), reads=["ACC"], writes=["carry"])
                                fw.op(dve, lambda e: e.tensor_copy(out=HB[R, :], in_=ACC[R, :]), reads=["ACC"], writes=["HB"])
                                fw.op(dve, lambda e: e.tensor_tensor(out=RR[R, :], in0=ACC[R, :], in1=HB[R, :], op=ALU.subtract),
                                      reads=["ACC", "HB"], writes=["RR"])
                                fw.op(dve, lambda e: e.tensor_copy(out=MB[R, :], in_=RR[R, :]), reads=["RR"], writes=["MB"])
                                fw.op(dve, lambda e: e.tensor_tensor(out=ACC[R, :], in0=RR[R, :], in1=MB[R, :], op=ALU.subtract),
                                      reads=["RR", "MB"], writes=["ACC"])
                                fw.op(dve, lambda e: e.tensor_copy(out=LB[R, :], in_=ACC[R, :]), reads=["ACC"], writes=["LB"])
                                for w_, c0 in ((0, 2), (1, 6)):
                                    ST = STs[w_][tt % 2]
                                    stk = f"ST{w_}{tt % 2}"
                                    fw.op(dve, lambda e, c0=c0: e.tensor_scalar(
                                        out=AQ[R, :], in0=HB[R, :], scalar1=cc[R, c0:c0 + 1], scalar2=cc[R, c0 + 3:c0 + 4],
                                        op0=ALU.mult, op1=ALU.add), reads=["HB", "c_cc"], writes=["AQ"])
                                    fw.op(dve, lambda e, c0=c0: e.scalar_tensor_tensor(
                                        out=RR[R, :], in0=MB[R, :], scalar=cc[R, c0 + 1:c0 + 2], in1=AQ[R, :],
                                        op0=ALU.mult, op1=ALU.add), reads=["MB", "AQ", "c_cc"], writes=["RR"])
                                    fw.op(dve, lambda e, c0=c0, ST=ST: e.scalar_tensor_tensor(
                                        out=ST[R, :], in0=LB[R, :], scalar=cc[R, c0 + 2:c0 + 3], in1=RR[R, :],
                                        op0=ALU.mult, op1=ALU.add), reads=["LB", "RR", "c_cc"], writes=[stk])
                                    for h in range(4):
                                        idx = w_ * 4 + h
                                        fw.dma(sp, f"aug{idx}_{tt % 2}", qk[idx][64:70, tsl(tt)], ST[h * 6:(h + 1) * 6, :],
                                               reads=[stk], writes=[f"qa{idx}_{tt}"])
                            qkproj(act_only=True)
                            attn_A(dtmps)
                            fw.barrier()
                    _attn()
                if m == 1:
                    def _attn():
                        with ExitStack() as _es:
                            for idx_ in range(8):
                                fw.op(pool, lambda e, idx_=idx_: e.memset(qk[idx_][64:128, :], 0.0), writes=[f"qk{idx_}"])
                            qkproj()
                            EB = _es.enter_context(SB("EB", [128, 4, 640], F32))
                            visb = _es.enter_context(SB("sb_visb", [128, 640], BF16))
                            tbs = [_es.enter_context(SB(f"tb{i}", [128, 512], F32)) for i in range(2)]
                            fw.dma(sp, "eb", EB[:], relb_d[l], writes=["EB"])
                            fw.dma(sp, "eb2", visb[:], visb_d.ap(), writes=["visb"])
                            fw.op(act, lambda e: e.activation(out=EB[:], in_=EB[:], func=AF.Exp), reads=["EB"], writes=["EB"])
                            for h in range(4):
                                fw.op(dve, lambda e, h=h: e.tensor_tensor(out=EB[:, h, :], in0=EB[:, h, :], in1=visb[:], op=ALU.mult),
                                      reads=["EB", "visb"], writes=["EB"])
                            attn_B(EB, tbs)
                            fw.barrier()
                    _attn()
                if m == 2:
                    def _attn():
                        with ExitStack() as _es:
                            qkproj()
                            c1s = [_es.enter_context(SB(f"c1{i}", [128, 512], F32)) for i in range(2)]
                            c2s = [_es.enter_context(SB(f"c2{i}", [128, 512], F32)) for i in range(2)]
                            csq = _es.enter_context(SB("csq", [128, 512], BF16))
                            cl = _es.enter_context(SB("cl", [128, 512], F32))
                            for h in range(4):
                                for which in range(2):
                                    dst = qk[which * 4 + h]
                                    for mp in range(2):
                                        fw.dma(sp, f"cau{which * 4 + h}", dst[mp * 64 + 32:mp * 64 + 36, :], caug_d[h, which],
                                               writes=[f"qk{which * 4 + h}"])
                            attn_C(c1s, c2s, csq, cl)
                            fw.barrier()
                    _attn()
                if m == 3:
                    def _attn():
                        with ExitStack() as _es:
                            for idx_ in range(8):
                                fw.op(pool, lambda e, idx_=idx_: e.memset(qk[idx_][64:128, :], 0.0), writes=[f"qk{idx_}"])
                            qkproj()
                            e32s = [_es.enter_context(SB(f"e32{i}", [128, 512], F32)) for i in range(2)]
                            l32s = [_es.enter_context(SB(f"l32{i}", [128, 512], F32)) for i in range(3)]
                            lbs = [_es.enter_context(SB(f"lb{i}", [128, 512], BF16)) for i in range(3)]
                            t1s = [_es.enter_context(SB(f"t1{i}", [128, 512], F32)) for i in range(2)]
                            attn_D(e32s, l32s, lbs, t1s)
                            fw.barrier()
                    _attn()

                for oc in range(8):
                    for tt in range(4):
                        bank = 5 + ctr["w"] % 2
                        ctr["w"] += 1
                        fw.op(pe, [lambda e, pr=pr, oc=oc, tt=tt, bank=bank: e.matmul(
                            pb[bank][:], lhsT=wo[:, pr, oc * 128:(oc + 1) * 128], rhs=oTs[pr][:, tsl(tt)],
                            start=(pr == 0), stop=(pr == 1)) for pr in range(2)],
                            reads=["sq", "oT0", "oT1"], writes=[f"pb{bank}"])
                        fw.op(dve, lambda e, oc=oc, tt=tt, bank=bank: e.tensor_tensor(
                            out=xT[:, oc, tsl(tt)], in0=pb[bank][:], in1=xT[:, oc, tsl(tt)], op=ALU.add),
                            reads=[f"pb{bank}", xk(oc, tt)], writes=[xk(oc, tt)])
                fw.barrier()

    if stages is None:
        stages = []
        for l in range(NL):
            stages += [("ffn", l, 0), ("mix", l), ("ffn", l, 1)]
    for b in range(nseq):
        load_x(b)
        for stg in stages:
            if stg[0] == "ffn":
                ffn(stg[1], stg[2])
            else:
                mixer(stg[1])
        store_x(b, final_norm)
    nc._fw_stats = (fw.n_inst, fw.n_wait)
    return nc


def _host_inputs(inputs):
    f32 = np.float32
    g = np.stack([inputs["g_ffn1"][0], inputs["g_mix"][0], inputs["g_ffn2"][0],
                  inputs["g_ffn1"][1], inputs["g_mix"][1], inputs["g_ffn2"][1], inputs["g_final"]], 0).astype(f32)
    gall = np.ascontiguousarray(g.reshape(7, 8, 128).transpose(2, 0, 1).reshape(128, 56))
    bfb = np.zeros((128, 2), f32)
    bfb[0:24, :] = np.repeat(inputs["b_f"].astype(f32).T, 6, axis=0)
    dlb = np.ascontiguousarray(np.broadcast_to(inputs["diff_lambda"].astype(f32).reshape(1, 256), (128, 256)))
    kk = np.arange(128)[:, None]
    qq = np.arange(640)[None, :]
    ridx = np.clip(qq - kk, -256, 256) + 256
    relb = np.ascontiguousarray(inputs["rel_bias"].astype(f32)[:, :, ridx].transpose(0, 2, 1, 3))
    m = dict(_consts())
    m.update(gall=gall, bfb=bfb, dlb=dlb, relb=relb)
    for k in ("ffn1_w_gu", "ffn2_w_gu", "ffn1_w_down", "ffn2_w_down", "w_in", "w_out"):
        m[k] = np.ascontiguousarray(inputs[k], dtype=f32)
    return m


def kernel(**inputs):
    x = np.ascontiguousarray(inputs["x"], dtype=np.float32)
    shared = _host_inputs(inputs)
    nc = build()
    in_maps = []
    for c in range(NCORES):
        d = dict(shared)
        d["x"] = x[c * NSEQ:(c + 1) * NSEQ]
        in_maps.append(d)
    res = run_bass_kernel_spmd(nc, in_maps, core_ids=list(range(NCORES)))
    return np.concatenate([np.asarray(r["y"], dtype=np.float32) for r in res.results], axis=0)
```

```python
import math
from contextlib import ExitStack
import numpy as np
import ml_dtypes
import concourse.bass as bass
import concourse.mybir as mybir
from concourse.bass_utils import run_bass_kernel_spmd

F32 = mybir.dt.float32
BF16 = mybir.dt.bfloat16
AF = mybir.ActivationFunctionType
ALU = mybir.AluOpType

S = 2048
D = 1024
DFF = 2816
NJ = 22
INW = 3076
NL = 2
EPS = 1e-6
EPOCH = 30000
NCORES = 8
NSEQ = 4
MIXERS = (0, 1, 2, 3)


class _Eng:
    def __init__(self, fw, name, handle, nsem):
        self.name = name
        self.h = handle
        self.sems = [fw.nc.alloc_semaphore(f"s_{name}_{i}") for i in range(nsem)]
        self.count = 0
        self.known = {}


class FW:
    def __init__(self, nc, nsem=4):
        self.nc = nc
        self.pe = _Eng(self, "pe", nc.tensor, nsem)
        self.act = _Eng(self, "act", nc.scalar, nsem)
        self.dve = _Eng(self, "dve", nc.vector, nsem)
        self.pool = _Eng(self, "pool", nc.gpsimd, nsem)
        self.sp = _Eng(self, "sp", nc.sync, 1)
        self.engs = {e.name: e for e in (self.pe, self.act, self.dve, self.pool, self.sp)}
        self.lastw = {}
        self.readers = {}
        self.dma_sems = {}
        self.n_wait = 0
        self.n_inst = 0

    def _wait(self, eng, ev):
        if ev is None:
            return
        if ev[0] == 'e':
            src = self.engs[ev[1]]
            idx = ev[2]
            if src is eng and eng is self.pe:
                return
            if eng.known.get(src.name, 0) >= idx + 1:
                return
            eng.h.wait_ge(src.sems[idx // EPOCH], idx % EPOCH + 1)
            self.n_wait += 1
            eng.known[src.name] = idx + 1
        else:
            st, val = ev[1], ev[2]
            key = ('d', st)
            if eng.known.get(key, 0) >= val:
                return
            eng.h.wait_ge(self.dma_sems[st][0], val)
            self.n_wait += 1
            eng.known[key] = val

    def _deps(self, eng, reads, writes):
        for k in reads:
            self._wait(eng, self.lastw.get(k))
        for k in writes:
            self._wait(eng, self.lastw.get(k))
            for ev in self.readers.get(k, ()):
                self._wait(eng, ev)

    def _commit(self, ev, reads, writes):
        for k in reads:
            if not k.startswith("c_"):
                self.readers.setdefault(k, []).append(ev)
        for k in writes:
            self.lastw[k] = ev
            self.readers[k] = []

    def op(self, eng, fns, reads=(), writes=()):
        self._deps(eng, reads, writes)
        if not isinstance(fns, (list, tuple)):
            fns = [fns]
        inst = None
        for fn in fns:
            inst = fn(eng.h)
        idx = eng.count
        inst.then_inc(eng.sems[idx // EPOCH], 1)
        eng.count += 1
        self.n_inst += len(fns)
        self._commit(('e', eng.name, idx), reads, writes)

    def dma(self, queue, stream, out, in_, reads=(), writes=(), **kw):
        if stream not in self.dma_sems:
            self.dma_sems[stream] = [self.nc.alloc_semaphore(f"d_{stream}"), 0]
        self._deps(queue, reads, writes)
        ds = self.dma_sems[stream]
        queue.h.dma_start(out=out, in_=in_, **kw).then_inc(ds[0], 16)
        ds[1] += 16
        self.n_inst += 1
        self._commit(('d', stream, ds[1]), reads, writes)

    def barrier(self):
        evs = []
        for e in (self.pe, self.act, self.dve, self.pool):
            if e.count:
                evs.append(('e', e.name, e.count - 1))
        for st, (sem, val) in self.dma_sems.items():
            if val:
                evs.append(('d', st, val))
        for e in (self.pe, self.act, self.dve, self.pool, self.sp):
            for ev in evs:
                self._wait(e, ev)
        self.lastw = {}
        self.readers = {}


def _consts():
    c = {}
    c["ident"] = np.eye(128, dtype=np.float32)
    kk = np.arange(128)[:, None]
    qq = np.arange(128)[None, :]
    bf = ml_dtypes.bfloat16
    m = np.zeros((128, 5, 128), np.float32)
    m[:, 0, :] = (qq >= kk)
    m[:, 1, :] = (qq > kk)
    m[:, 2, :] = (kk > qq)
    m[:, 3, :] = (kk <= qq)
    m[:, 4, :] = 1.0
    c["masks"] = m.astype(bf)
    slopes = [2.0 ** (-8.0 * (i + 1) / 4) for i in range(4)]
    mc = np.zeros((128, 5, 128), np.float32)
    vis = ((kk // 64) <= (qq // 64))
    for h in range(4):
        corr = np.where(kk > qq, np.exp(-2.0 * slopes[h] * (kk - qq).astype(np.float64)), 1.0)
        mc[:, h, :] = np.where(vis, corr, 0.0)
    mc[:, 4, :] = np.where(qq >= kk, 0.0, -30000.0)
    c["maskc"] = mc
    t = np.arange(S)
    caug = np.zeros((4, 2, 4, S), np.float32)
    for h in range(4):
        sl = slopes[h]
        caug[h, 0, 0] = -sl * 64 * (t // 64)
        caug[h, 0, 1] = -sl * (t % 64)
        caug[h, 0, 2] = 1.0
        caug[h, 0, 3] = 1.0
        caug[h, 1, 0] = 1.0
        caug[h, 1, 1] = 1.0
        caug[h, 1, 2] = sl * 64 * (t // 64)
        caug[h, 1, 3] = sl * (t % 64)
    c["caug"] = caug.astype(bf)
    kk6 = np.arange(128)[:, None]
    qq6 = np.arange(640)[None, :]
    visb = ((kk6 // 64) <= (qq6 // 64)) & ((qq6 // 64) <= (kk6 // 64) + 8)
    c["visb"] = visb.astype(np.float32).astype(bf)
    cc = np.zeros((128, 16), np.float32)
    cc[:, 0] = EPS
    cc[:, 1] = 1.0
    for p in range(24):
        i = p % 6
        if i == 0: cc[p, 2] = -1.0
        if i == 1: cc[p, 3] = -1.0
        if i == 2: cc[p, 4] = -1.0
        if i >= 3: cc[p, 5] = 1.0
        if i == 3: cc[p, 6] = 1.0
        if i == 4: cc[p, 7] = 1.0
        if i == 5: cc[p, 8] = 1.0
        if i < 3: cc[p, 9] = 1.0
    c["cc"] = cc
    return c


def _lam_init(l):
    return 0.8 - 0.6 * math.exp(-0.3 * l)


def build(nseq=NSEQ, stages=None, final_norm=True):
    nc = bass.Bass("TRN2", target_bir_lowering=False)
    fw = FW(nc)
    pe, act, dve, pool, sp = fw.pe, fw.act, fw.dve, fw.pool, fw.sp

    uid = [0]

    def SB(name, shape, dt):
        uid[0] += 1
        return nc.sbuf_tensor(f"{name}_{uid[0]}", shape, dt)

    def din(name, shape, dt=F32):
        return nc.dram_tensor(name, list(shape), dt, kind="ExternalInput")

    x_d = din("x", [nseq, S, D])
    wgu_d = [din("ffn1_w_gu", [NL, D, 2 * DFF]), din("ffn2_w_gu", [NL, D, 2 * DFF])]
    wd_d = [din("ffn1_w_down", [NL, DFF, D]), din("ffn2_w_down", [NL, DFF, D])]
    win_d = din("w_in", [NL, D, INW])
    wout_d = din("w_out", [NL, D, D])
    gall_d = din("gall", [128, 56])
    bfb_d = din("bfb", [128, 2])
    dlb_d = din("dlb", [128, 256])
    relb_d = din("relb", [NL, 128, 4, 640])
    ident_d = din("ident", [128, 128])
    masks_d = din("masks", [128, 5, 128], BF16)
    maskc_d = din("maskc", [128, 5, 128])
    caug_d = din("caug", [4, 2, 4, S], BF16)
    visb_d = din("visb", [128, 640], BF16)
    cc_d = din("cc", [128, 16])
    y_d = nc.dram_tensor("y", [nseq, S, D], F32, kind="ExternalOutput")

    wgu_s = nc.dram_tensor("wgu_s", [NL, 2, NJ, 128, 2048], BF16, kind="Internal")
    wd_s = nc.dram_tensor("wd_s", [NL, 2, 8, 128, NJ * 128], BF16, kind="Internal")
    win_s = nc.dram_tensor("win_s", [NL, D, INW], BF16, kind="Internal")
    wout_s = nc.dram_tensor("wout_s", [NL, D, D], BF16, kind="Internal")

    pb = [nc.alloc_psum_tensor(f"pb{i}", [128, 512], F32) for i in range(8)]

    ident = nc.alloc_sbuf_tensor("sb_ident", [128, 128], F32)
    masks = nc.alloc_sbuf_tensor("sb_masks", [128, 5, 128], BF16)
    maskc = nc.alloc_sbuf_tensor("sb_maskc", [128, 5, 128], F32)
    cc = nc.alloc_sbuf_tensor("sb_cc", [128, 16], F32)
    gall = nc.alloc_sbuf_tensor("sb_gall", [128, 56], F32)
    bfb = nc.alloc_sbuf_tensor("sb_bfb", [128, 2], F32)
    lamc = nc.alloc_sbuf_tensor("lamc", [128, 8], F32)
    ones_b = masks[:, 4, :]
    mA = masks[:, 0, :]
    mD = masks[:, 1, :]
    mU = masks[:, 2, :]
    mLI = masks[:, 3, :]

    for (t, d, k) in ((ident, ident_d, "c_ident"), (masks, masks_d, "c_masks"), (maskc, maskc_d, "c_maskc"),
                      (cc, cc_d, "c_cc"), (gall, gall_d, "c_gall"), (bfb, bfb_d, "c_bfb")):
        fw.dma(sp, "cst_" + k, t[:], d.ap(), writes=[k])

    with ExitStack() as _es:
        dl = _es.enter_context(SB("dl", [128, 256], F32))
        dlt = _es.enter_context(SB("dlt", [128, 64], F32))
        dls = _es.enter_context(SB("dls", [128, 8], F32))
        fw.dma(sp, "cst_dl", dl[:], dlb_d.ap(), writes=["dl"])
        for l in range(NL):
            for pr in range(2):
                a0 = l * 128 + pr * 64
                fw.op(dve, lambda e, a0=a0, pr=pr: e.tensor_tensor(out=dlt[:, pr * 32:pr * 32 + 32], in0=dl[:, a0:a0 + 32],
                                                                  in1=dl[:, a0 + 32:a0 + 64], op=ALU.mult),
                      reads=["dl"], writes=["dlt"])
                fw.op(dve, lambda e, l=l, pr=pr: e.reduce_sum(out=dls[:, l * 2 + pr:l * 2 + pr + 1],
                                                              in_=dlt[:, pr * 32:pr * 32 + 32], axis=mybir.AxisListType.X),
                      reads=["dlt"], writes=["dls"])
        fw.op(act, lambda e: e.activation(out=dls[:, 4:8], in_=dls[:, 0:4], func=AF.Exp), reads=["dls"], writes=["dls"])
        for l in range(NL):
            fw.op(dve, lambda e, l=l: e.tensor_tensor(out=lamc[:, l:l + 1], in0=dls[:, 4 + 2 * l + 1:4 + 2 * l + 2],
                                                      in1=dls[:, 4 + 2 * l:4 + 2 * l + 1], op=ALU.subtract),
                  reads=["dls"], writes=["c_lamc"])
            fw.op(dve, lambda e, l=l: e.tensor_scalar(out=lamc[:, l:l + 1], in0=lamc[:, l:l + 1], scalar1=-_lam_init(l),
                                                      scalar2=None, op0=ALU.add),
                  reads=["c_lamc"], writes=["c_lamc"])
        fw.barrier()

    def prepass():
        with ExitStack() as _es:
            f0 = _es.enter_context(SB("ppf0", [128, 5632], F32))
            f1 = _es.enter_context(SB("ppf1", [128, 5632], F32))
            b0 = _es.enter_context(SB("ppb0", [128, 5632], BF16))
            b1 = _es.enter_context(SB("ppb1", [128, 5632], BF16))
            fbs, bbs = (f0, f1), (b0, b1)
            items = []
            for l in range(NL):
                for f in range(2):
                    for kc in range(8):
                        def st(bt, i, l=l, f=f, kc=kc):
                            dst = bass.AP(wgu_s, ((l * 2 + f) * NJ) * 128 * 2048 + kc * 256,
                                          [[2048, 128], [128 * 2048, NJ], [1, 256]])
                            src = bt[:, 0:2 * DFF].rearrange("p (j c) -> p j c", c=256)
                            fw.dma(sp, f"pps{i % 2}", dst, src, reads=[f"ppb{i % 2}"], writes=["scr"])
                        items.append((wgu_d[f][l, kc * 128:(kc + 1) * 128, :], 2 * DFF, st, "gu"))
                    for j in range(0, NJ, 2):
                        def st(bt, i, l=l, f=f, j=j):
                            dst = bass.AP(wd_s, ((l * 2 + f) * 8) * 128 * NJ * 128 + j * 128,
                                          [[NJ * 128, 128], [128 * NJ * 128, 8], [1, 256]])
                            src = bt[:, 0:2 * D].rearrange("p (o c) -> p o c", c=256)
                            fw.dma(sp, f"pps{i % 2}", dst, src, reads=[f"ppb{i % 2}"], writes=["scr"])
                        items.append((wd_d[f][l, j * 128:(j + 2) * 128, :].rearrange("(j p) d -> p j d", p=128), 2 * D, st, "d"))
                for kc in range(8):
                    def st(bt, i, l=l, kc=kc):
                        fw.dma(sp, f"pps{i % 2}", win_s[l, kc * 128:(kc + 1) * 128, :], bt[:, 0:INW], reads=[f"ppb{i % 2}"], writes=["scr"])
                    items.append((win_d[l, kc * 128:(kc + 1) * 128, :], INW, st, None))

                    def st2(bt, i, l=l, kc=kc):
                        fw.dma(sp, f"pps{i % 2}", wout_s[l, kc * 128:(kc + 1) * 128, :], bt[:, 0:D], reads=[f"ppb{i % 2}"], writes=["scr"])
                    items.append((wout_d[l, kc * 128:(kc + 1) * 128, :], D, st2, None))

            def load(i):
                src, C, _, kind = items[i]
                dstb = fbs[i % 2][:, 0:C]
                if kind == "d":
                    dstb = dstb.rearrange("p (j d) -> p j d", j=2)
                fw.dma(sp, f"ppl{i % 2}", dstb, src, writes=[f"ppf{i % 2}"])
            load(0)
            for i, (src, C, st, kind) in enumerate(items):
                if i + 1 < len(items):
                    load(i + 1)
                fb, bb = fbs[i % 2], bbs[i % 2]
                if kind == "gu":
                    ci = fb[:, 0:C].rearrange("p (h j c) -> p h j c", h=2, c=128)
                    co = bb[:, 0:C].rearrange("p (j h c) -> p h j c", h=2, c=128)
                elif kind == "d":
                    ci = fb[:, 0:C].rearrange("p (j o c) -> p j o c", j=2, c=128)
                    co = bb[:, 0:C].rearrange("p (o j c) -> p j o c", j=2, c=128)
                else:
                    ci, co = fb[:, 0:C], bb[:, 0:C]
                k = i % 3
                if k == 0:
                    fw.op(pool, lambda e, ci=ci, co=co: e.tensor_copy(out=co, in_=ci),
                          reads=[f"ppf{i % 2}"], writes=[f"ppb{i % 2}"])
                elif k == 1:
                    fw.op(dve, lambda e, ci=ci, co=co: e.tensor_copy(out=co, in_=ci),
                          reads=[f"ppf{i % 2}"], writes=[f"ppb{i % 2}"])
                else:
                    fw.op(act, lambda e, ci=ci, co=co: e.activation(out=co, in_=ci, func=AF.Copy),
                          reads=[f"ppf{i % 2}"], writes=[f"ppb{i % 2}"])
                st(bb, i)
            fw.barrier()

    prepass()

    xT = nc.alloc_sbuf_tensor("xT", [128, 8, S], F32)
    hT = nc.alloc_sbuf_tensor("hT", [128, 8, S], BF16)
    sq = nc.alloc_sbuf_tensor("sq", [128, 8, 512], BF16)
    lnv = nc.alloc_sbuf_tensor("lnv", [128, 512], F32)
    rstd = nc.alloc_sbuf_tensor("rstd", [128, 512], F32)

    def xk(fc, tt):
        return f"x{fc}_{tt}"

    def tsl(tt):
        return slice(tt * 512, (tt + 1) * 512)

    def norm_stats(tt, bank):
        fw.op(act, lambda e: e.activation(out=sq[:], in_=xT[:, :, tsl(tt)], func=AF.Square),
              reads=[xk(fc, tt) for fc in range(8)], writes=["sq"])
        fw.op(pe, [lambda e, fc=fc: e.matmul(pb[bank][:], lhsT=ones_b, rhs=sq[:, fc, :], start=(fc == 0), stop=(fc == 7))
                   for fc in range(8)], reads=["sq", "c_masks"], writes=[f"pb{bank}"])
        fw.op(act, lambda e: e.activation(out=lnv[:], in_=pb[bank][:], func=AF.Ln, bias=cc[:, 0:1], scale=1.0 / D),
              reads=[f"pb{bank}", "c_cc"], writes=["lnv"])
        fw.op(act, lambda e: e.activation(out=rstd[:], in_=lnv[:], func=AF.Exp, scale=-0.5),
              reads=["lnv"], writes=["rstd"])

    def norm_to_h(gidx, tt, hcol0, bank=6):
        norm_stats(tt, bank)
        for fc in range(8):
            fw.op(dve, lambda e, fc=fc: e.scalar_tensor_tensor(
                out=hT[:, fc, hcol0:hcol0 + 512], in0=xT[:, fc, tsl(tt)], scalar=gall[:, gidx * 8 + fc:gidx * 8 + fc + 1],
                in1=rstd[:], op0=ALU.mult, op1=ALU.mult),
                reads=[xk(fc, tt), "rstd", "c_gall"], writes=[f"h{hcol0 // 512}"])

    def load_x(b):
        with ExitStack() as _es:
            xts = [_es.enter_context(SB(f"xtok{i}", [128, D], F32)) for i in range(4)]
            for tc in range(16):
                xt = xts[tc % 4]
                fw.dma(sp, f"xl{tc % 4}", xt[:], x_d[b, tc * 128:(tc + 1) * 128, :], writes=[f"xtok{tc % 4}"])
                for g in range(2):
                    bank = 6 + g
                    fw.op(pe, [lambda e, q=q, g=g, xt=xt, bank=bank: e.transpose(
                        pb[bank][:, q * 128:(q + 1) * 128], xt[:, (g * 4 + q) * 128:(g * 4 + q + 1) * 128], ident[:])
                        for q in range(4)], reads=[f"xtok{tc % 4}", "c_ident"], writes=[f"pb{bank}"])
                    src = pb[bank][:].rearrange("p (q c) -> p q c", c=128)
                    dst = xT[:, g * 4:(g + 1) * 4, tc * 128:(tc + 1) * 128]
                    wk = [xk(fc, tc // 4) for fc in range(g * 4, g * 4 + 4)]
                    if g == 0:
                        fw.op(act, lambda e, src=src, dst=dst: e.activation(out=dst, in_=src, func=AF.Copy),
                              reads=[f"pb{bank}"], writes=wk)
                    else:
                        fw.op(dve, lambda e, src=src, dst=dst: e.tensor_copy(out=dst, in_=src),
                              reads=[f"pb{bank}"], writes=wk)
            fw.barrier()

    def store_x(b, do_norm):
        with ExitStack() as _es:
            yT = _es.enter_context(SB("yT", [128, 8, 512], F32))
            yts = [_es.enter_context(SB(f"ytok{i}", [128, D], F32)) for i in range(4)]
            cnt = 0
            for tt in range(4):
                if do_norm:
                    norm_stats(tt, 5)
                    for fc in range(8):
                        fw.op(dve, lambda e, fc=fc: e.scalar_tensor_tensor(
                            out=yT[:, fc, :], in0=xT[:, fc, tsl(tt)], scalar=gall[:, 48 + fc:48 + fc + 1],
                            in1=rstd[:], op0=ALU.mult, op1=ALU.mult),
                            reads=[xk(fc, tt), "rstd", "c_gall"], writes=["yT"])
                else:
                    fw.op(dve, lambda e: e.tensor_copy(out=yT[:], in_=xT[:, :, tsl(tt)]),
                          reads=[xk(fc, tt) for fc in range(8)], writes=["yT"])
                for tcl in range(4):
                    yt = yts[cnt % 4]
                    for g in range(2):
                        bank = 6 + g
                        fw.op(pe, [lambda e, q=q, g=g, bank=bank, tcl=tcl: e.transpose(
                            pb[bank][:, q * 128:(q + 1) * 128], yT[:, g * 4 + q, tcl * 128:(tcl + 1) * 128], ident[:])
                            for q in range(4)], reads=["yT", "c_ident"], writes=[f"pb{bank}"])
                        if g == 0:
                            fw.op(act, lambda e, yt=yt, bank=bank: e.activation(out=yt[:, 0:512], in_=pb[bank][:], func=AF.Copy),
                                  reads=[f"pb{bank}"], writes=[f"ytok{cnt % 4}a"])
                        else:
                            fw.op(dve, lambda e, yt=yt, bank=bank: e.tensor_copy(out=yt[:, 512:1024], in_=pb[bank][:]),
                                  reads=[f"pb{bank}"], writes=[f"ytok{cnt % 4}b"])
                    r0 = tt * 512 + tcl * 128
                    fw.dma(sp, f"yst{cnt % 4}", y_d[b, r0:r0 + 128, :], yt[:],
                           reads=[f"ytok{cnt % 4}a", f"ytok{cnt % 4}b"], writes=["y_out"])
                    cnt += 1
            fw.barrier()

    def ffn(l, f):
        gidx = l * 3 + (0 if f == 0 else 2)
        with ExitStack() as _es:
            actT = _es.enter_context(SB("actT", [128, NJ, 1024], BF16))
            wg0 = _es.enter_context(SB("wgb0", [128, 8, 256], BF16))
            wg1 = _es.enter_context(SB("wgb1", [128, 8, 256], BF16))
            wg2 = _es.enter_context(SB("wgb2", [128, 8, 256], BF16))
            wdb0 = _es.enter_context(SB("wdb0", [128, NJ, 128], BF16))
            wdb1 = _es.enter_context(SB("wdb1", [128, NJ, 128], BF16))
            sg0 = _es.enter_context(SB("sg0", [128, 512], F32))
            sg1 = _es.enter_context(SB("sg1", [128, 512], F32))
            wgs = (wg0, wg1, wg2)
            wds = (wdb0, wdb1)
            sgs = (sg0, sg1)
            st = {"g": 0, "d": 0, "wg": 0, "wd": 0}

            def do_norm(half):
                for t2 in range(2):
                    norm_to_h(gidx, half * 2 + t2, half * 1024 + t2 * 512)

            def do_gu(half):
                for j in range(NJ):
                    wi = st["wg"] % 3
                    st["wg"] += 1
                    wg = wgs[wi]
                    fw.dma(sp, f"wg{wi}", wg[:].rearrange("p a b -> p (a b)"), wgu_s[l, f, j], reads=["scr"], writes=[f"wg{wi}"])
                    for t2 in range(2):
                        c = st["g"] % 2
                        st["g"] += 1
                        gb, ub = 2 * c, 2 * c + 1
                        hc = half * 1024 + t2 * 512
                        hk = f"h{hc // 512}"
                        fw.op(pe, [lambda e, kc=kc, wg=wg, gb=gb, hc=hc: e.matmul(
                            pb[gb][:], lhsT=wg[:, kc, 0:128], rhs=hT[:, kc, hc:hc + 512], start=(kc == 0), stop=(kc == 7))
                            for kc in range(8)], reads=[f"wg{wi}", hk], writes=[f"pb{gb}"])
                        fw.op(pe, [lambda e, kc=kc, wg=wg, ub=ub, hc=hc: e.matmul(
                            pb[ub][:], lhsT=wg[:, kc, 128:256], rhs=hT[:, kc, hc:hc + 512], start=(kc == 0), stop=(kc == 7))
                            for kc in range(8)], reads=[f"wg{wi}", hk], writes=[f"pb{ub}"])
                        sg = sgs[c]
                        fw.op(act, lambda e, sg=sg, gb=gb: e.activation(out=sg[:], in_=pb[gb][:], func=AF.Silu),
                              reads=[f"pb{gb}"], writes=[f"sg{c}"])
                        fw.op(dve, lambda e, sg=sg, ub=ub, j=j, t2=t2: e.tensor_tensor(
                            out=actT[:, j, t2 * 512:(t2 + 1) * 512], in0=pb[ub][:], in1=sg[:], op=ALU.mult),
                            reads=[f"pb{ub}", f"sg{c}"], writes=[f"a{j}_{t2}"])

            def do_down(half):
                for oc in range(8):
                    wi = st["wd"] % 2
                    st["wd"] += 1
                    wd = wds[wi]
                    fw.dma(sp, f"wd{wi}", wd[:].rearrange("p a b -> p (a b)"), wd_s[l, f, oc], reads=["scr"], writes=[f"wd{wi}"])
                    for t2 in range(2):
                        bank = 4 + st["d"] % 2
                        st["d"] += 1
                        tt = half * 2 + t2
                        fw.op(pe, [lambda e, j=j, wd=wd, bank=bank, t2=t2: e.matmul(
                            pb[bank][:], lhsT=wd[:, j, :], rhs=actT[:, j, t2 * 512:(t2 + 1) * 512], start=(j == 0), stop=(j == NJ - 1))
                            for j in range(NJ)], reads=[f"wd{wi}"] + [f"a{j}_{t2}" for j in range(NJ)], writes=[f"pb{bank}"])
                        fw.op(dve, lambda e, bank=bank, oc=oc, tt=tt: e.scalar_tensor_tensor(
                            out=xT[:, oc, tsl(tt)], in0=pb[bank][:], scalar=0.5, in1=xT[:, oc, tsl(tt)],
                            op0=ALU.mult, op1=ALU.add), reads=[f"pb{bank}", xk(oc, tt)], writes=[xk(oc, tt)])

            do_norm(0)
            do_gu(0)
            do_norm(1)
            do_down(0)
            do_gu(1)
            do_down(1)
            fw.barrier()

    def mixer(l):
        lam0 = _lam_init(l)
        with ExitStack() as _es:
            wbuf = _es.enter_context(SB("wbuf", [128, 8, 772], BF16))
            t0 = _es.enter_context(SB("qk0", [128, S], BF16))
            t1 = _es.enter_context(SB("qk1", [128, S], BF16))
            t2_ = _es.enter_context(SB("qk2", [128, S], BF16))
            t3 = _es.enter_context(SB("qk3", [128, S], BF16))
            t4 = _es.enter_context(SB("qk4", [128, S], BF16))
            t5 = _es.enter_context(SB("qk5", [128, S], BF16))
            t6 = _es.enter_context(SB("qk6", [128, S], BF16))
            t7 = _es.enter_context(SB("qk7", [128, S], BF16))
            Vt = _es.enter_context(SB("Vt", [128, 16, 4, 128], BF16))
            oT0 = _es.enter_context(SB("oT0", [128, S], BF16))
            oT1 = _es.enter_context(SB("oT1", [128, S], BF16))
            Pts = [_es.enter_context(SB(f"Pt{i}", [128, 512], BF16)) for i in range(6)]
            rb0 = _es.enter_context(SB("rb0", [128, 512], F32))
            rb1 = _es.enter_context(SB("rb1", [128, 512], F32))
            qk = (t0, t1, t2_, t3, t4, t5, t6, t7)
            oTs = (oT0, oT1)
            rbs = (rb0, rb1)
            ctr = {"s": 0, "p": 0, "o": 0, "pj": 0, "v": 0, "w": 0, "r": 0}

            def next_s():
                ctr["s"] += 1
                return ctr["s"] % 3

            def next_p():
                ctr["p"] += 1
                return ctr["p"] % 3

            def load_win(m_):
                base_ = (0, 772, 1540, 2308)[m_]
                ncols_ = 772 if m_ == 0 else 768
                fw.dma(sp, "wb", wbuf[:, :, 0:ncols_],
                       bass.AP(win_s, l * D * INW + base_, [[INW, 128], [128 * INW, 8], [1, ncols_]]),
                       reads=["scr"], writes=["wbuf"])

            load_win(MIXERS[0])
            for tt in range(4):
                norm_to_h(l * 3 + 1, tt, tt * 512)
            wo = bass.AP(sq, 0, [[8 * 512, 128], [1024, 2], [1, 1024]])
            for mi, m in enumerate(MIXERS):
                qscale = (0.125, 0.125, 32 ** -0.5, 0.125)[m]
                fw.dma(sp, "wo", wo, bass.AP(wout_s, l * D * D + m * 256 * D, [[D, 128], [128 * D, 2], [1, D]]),
                       reads=["scr"], writes=["sq"])
                nxt = MIXERS[mi + 1] if mi + 1 < len(MIXERS) else None
                fw.op(pool, lambda e: e.memset(Vt[:, :, :, 64:128], 1.0), writes=["Vt"])
                def vproj():
                    for tc in range(16):
                        bank = 3 + ctr["v"] % 2
                        ctr["v"] += 1
                        fw.op(pe, [lambda e, kc=kc, tc=tc, bank=bank: e.matmul(
                            pb[bank][:, 0:256], lhsT=hT[:, kc, tc * 128:(tc + 1) * 128], rhs=wbuf[:, kc, 512:768],
                            start=(kc == 0), stop=(kc == 7)) for kc in range(8)],
                            reads=["wbuf", f"h{tc // 4}"], writes=[f"pb{bank}"])
                        src = pb[bank][:, 0:256].rearrange("p (h c) -> p h c", c=64)
                        if tc % 2 == 0:
                            fw.op(act, lambda e, src=src, tc=tc: e.activation(out=Vt[:, tc, :, 0:64], in_=src, func=AF.Copy),
                                  reads=[f"pb{bank}"], writes=["Vt"])
                        else:
                            fw.op(dve, lambda e, src=src, tc=tc: e.tensor_copy(out=Vt[:, tc, :, 0:64], in_=src),
                                  reads=[f"pb{bank}"], writes=["Vt"])
                    if nxt is not None:
                        load_win(nxt)
                def qkproj(act_only=False):
                    for which in range(2):
                        for pair in range(2):
                            c0 = which * 256 + pair * 128
                            for tt in range(4):
                                bank = ctr["pj"] % 2
                                ctr["pj"] += 1
                                fw.op(pe, [lambda e, kc=kc, c0=c0, tt=tt, bank=bank: e.matmul(
                                    pb[bank][:], lhsT=wbuf[:, kc, c0:c0 + 128], rhs=hT[:, kc, tsl(tt)],
                                    start=(kc == 0), stop=(kc == 7)) for kc in range(8)],
                                    reads=["wbuf", f"h{tt}"], writes=[f"pb{bank}"])
                                sc = qscale if which == 0 else 1.0
                                if m != 2:
                                    pieces = [(0, 64, qk[which * 4 + 2 * pair], 0), (64, 64, qk[which * 4 + 2 * pair + 1], 0)]
                                else:
                                    ta, tb = qk[which * 4 + 2 * pair], qk[which * 4 + 2 * pair + 1]
                                    pieces = [(0, 32, ta, 0), (32, 32, ta, 64), (64, 32, tb, 0), (96, 32, tb, 64)]
                                for pi, (r0, nr, dst, d0) in enumerate(pieces):
                                    hidx = qk.index(dst)
                                    if pi % 2 == 0 or act_only:
                                        fw.op(act, lambda e, r0=r0, nr=nr, dst=dst, d0=d0, bank=bank, tt=tt, sc=sc: e.activation(
                                            out=dst[d0:d0 + nr, tsl(tt)], in_=pb[bank][r0:r0 + nr, :], func=AF.Copy, scale=sc),
                                            reads=[f"pb{bank}"], writes=[f"qk{hidx}"])
                                    else:
                                        fw.op(dve, lambda e, r0=r0, nr=nr, dst=dst, d0=d0, bank=bank, tt=tt, sc=sc: e.tensor_scalar(
                                            out=dst[d0:d0 + nr, tsl(tt)], in0=pb[bank][r0:r0 + nr, :], scalar1=sc, scalar2=None,
                                            op0=ALU.mult), reads=[f"pb{bank}"], writes=[f"qk{hidx}"])

                def finalize_softmax(h, qt, ob):
                    ri = ctr["r"] % 2
                    ctr["r"] += 1
                    rb = rbs[ri]
                    fw.op(act, lambda e: e.activation(out=rb[64:128, :], in_=pb[ob][64:128, :], func=AF.Ln),
                          reads=[f"pb{ob}"], writes=[f"rb{ri}"])
                    fw.op(act, lambda e: e.activation(out=rb[64:128, :], in_=rb[64:128, :], func=AF.Exp, scale=-1.0),
                          reads=[f"rb{ri}"], writes=[f"rb{ri}"])
                    dst = oTs[h // 2][(h % 2) * 64:(h % 2) * 64 + 64, tsl(qt)]
                    fw.op(dve, lambda e: e.tensor_tensor(out=dst, in0=pb[ob][0:64, :], in1=rb[64:128, :], op=ALU.mult),
                          reads=[f"pb{ob}", f"rb{ri}"], writes=[f"oT{h // 2}"])

                def run_pipe(steps, LA=2, mid=None):
                    n = len(steps)
                    for i in range(n + LA):
                        if i == LA and mid is not None:
                            mid()
                        if i < n:
                            steps[i][0]()
                        if i >= LA:
                            steps[i - LA][1]()

                def pv_back(h, j, n, c0, ob, first, last, pi, fin):
                    Pt = Pts[pi]

                    def back():
                        fw.op(pe, lambda e: e.matmul(
                            pb[ob][:, c0:c0 + n], lhsT=Vt[:, j, h, :], rhs=Pt[:, 0:n], start=first, stop=last,
                            skip_group_check=True), reads=["Vt", f"Pt{pi}"], writes=[f"pb{ob}"])
                        if last and fin is not None:
                            fin()
                    return back

                def attn_A(dtmps):
                    steps = []
                    gi = 0
                    for h in range(4):
                        qa, ka = qk[h], qk[4 + h]
                        for qt in range(4):
                            ob = 3 + ctr["o"] % 2
                            ctr["o"] += 1
                            jmax = 4 * qt + 3
                            for j in range(jmax + 1):
                                q0 = max(j, 4 * qt) * 128
                                n = (4 * qt + 4) * 128 - q0
                                c0 = q0 - qt * 512
                                sb = (0, 1, 2, 5, 6)[gi % 5]
                                pi = gi % 6
                                di = gi % 2
                                gi += 1
                                diag = j >= 4 * qt

                                def front(h=h, qa=qa, ka=ka, j=j, q0=q0, n=n, sb=sb, pi=pi, di=di, diag=diag):
                                    Pt = Pts[pi]
                                    fw.op(pe, lambda e: e.matmul(
                                        pb[sb][:, 0:n], lhsT=ka[0:70, j * 128:(j + 1) * 128], rhs=qa[0:70, q0:q0 + n],
                                        start=True, stop=True),
                                        reads=[f"qk{h}", f"qk{4 + h}"] + [f"qa{h}_{t_}" for t_ in range(4)] + [f"qa{4 + h}_{t_}" for t_ in range(4)],
                                        writes=[f"pb{sb}"])
                                    if diag:
                                        dtmp = dtmps[di]
                                        fw.op(dve, lambda e: e.tensor_tensor(out=dtmp[:, 0:128], in0=pb[sb][:, 0:128], in1=maskc[:, 4, :], op=ALU.add),
                                              reads=[f"pb{sb}", "c_maskc"], writes=[f"dtmp{di}"])
                                        fw.op(act, lambda e: e.activation(out=Pt[:, 0:128], in_=dtmp[:, 0:128], func=AF.Exp),
                                              reads=[f"dtmp{di}"], writes=[f"Pt{pi}"])
                                        if n > 128:
                                            fw.op(act, lambda e: e.activation(out=Pt[:, 128:n], in_=pb[sb][:, 128:n], func=AF.Exp),
                                                  reads=[f"pb{sb}"], writes=[f"Pt{pi}"])
                                    else:
                                        fw.op(act, lambda e: e.activation(out=Pt[:, 0:n], in_=pb[sb][:, 0:n], func=AF.Exp),
                                              reads=[f"pb{sb}"], writes=[f"Pt{pi}"])
                                fin = (lambda h=h, qt=qt, ob=ob: finalize_softmax(h, qt, ob))
                                steps.append((front, pv_back(h, j, n, c0, ob, j == 0, j == jmax, pi, fin)))
                    run_pipe(steps, LA=5, mid=vproj)

                def attn_B(EB, tbs):
                    steps = []
                    gi = 0
                    for h in range(4):
                        qa, ka = qk[h], qk[4 + h]
                        for qt in range(4):
                            ob = 3 + ctr["o"] % 2
                            ctr["o"] += 1
                            js = [4 * qt] + [j for j in range(max(0, 4 * qt - 4), 4 * qt + 4) if j != 4 * qt]
                            for idx, j in enumerate(js):
                                i0 = max(j, 4 * qt)
                                i1 = min(j + 4, 4 * qt + 3)
                                q0 = i0 * 128
                                n = (i1 - i0 + 1) * 128
                                eoff = (i0 - j) * 128
                                c0 = q0 - qt * 512
                                sb = (0, 1, 2, 5, 6)[gi % 5]
                                pi = gi % 6
                                ti = gi % 2
                                gi += 1

                                def front(h=h, qa=qa, ka=ka, j=j, q0=q0, n=n, sb=sb, pi=pi, ti=ti, eoff=eoff):
                                    Pt = Pts[pi]
                                    tb = tbs[ti]
                                    fw.op(pe, lambda e: e.matmul(
                                        pb[sb][:, 0:n], lhsT=ka[:, j * 128:(j + 1) * 128], rhs=qa[:, q0:q0 + n],
                                        start=True, stop=True), reads=[f"qk{h}", f"qk{4 + h}"], writes=[f"pb{sb}"])
                                    fw.op(act, lambda e: e.activation(out=tb[:, 0:n], in_=pb[sb][:, 0:n], func=AF.Exp),
                                          reads=[f"pb{sb}"], writes=[f"tb{ti}"])
                                    fw.op(dve, lambda e: e.tensor_tensor(
                                        out=Pt[:, 0:n], in0=tb[:, 0:n], in1=EB[:, h, eoff:eoff + n], op=ALU.mult),
                                        reads=[f"tb{ti}", "EB"], writes=[f"Pt{pi}"])
                                fin = (lambda h=h, qt=qt, ob=ob: finalize_softmax(h, qt, ob))
                                steps.append((front, pv_back(h, j, n, c0, ob, idx == 0, idx == len(js) - 1, pi, fin)))
                    run_pipe(steps, LA=5, mid=vproj)

                def attn_C(c1s, c2s, csq, cl):
                    steps = []
                    gi = 0
                    tile_i = 0
                    H = slice(0, 64)
                    for h in range(4):
                        qa, ka = qk[h], qk[4 + h]
                        for qt in range(4):
                            obs = (3, 4) if tile_i % 2 == 0 else (5, 6)
                            c1, c2 = c1s[tile_i % 2], c2s[tile_i % 2]
                            ck = tile_i % 2
                            tile_i += 1
                            jmax = 4 * qt + 3

                            def fin(h=h, qt=qt, obs=obs, c1=c1, c2=c2, ck=ck):
                                o1, o2 = obs
                                for (ob, rb, rk) in ((o1, rb0, "rb0"), (o2, rb1, "rb1")):
                                    fw.op(act, lambda e, ob=ob, rb=rb: e.activation(out=rb[64:128, :], in_=pb[ob][64:128, :], func=AF.Ln),
                                          reads=[f"pb{ob}"], writes=[rk])
                                    fw.op(act, lambda e, rb=rb: e.activation(out=rb[64:128, :], in_=rb[64:128, :], func=AF.Exp, scale=-1.0),
                                          reads=[rk], writes=[rk])
                                fw.op(dve, lambda e: e.tensor_tensor(out=c1[H, :], in0=pb[o1][H, :], in1=rb0[64:128, :], op=ALU.mult),
                                      reads=[f"pb{o1}", "rb0"], writes=[f"c1{ck}"])
                                fw.op(dve, lambda e: e.tensor_tensor(out=c2[H, :], in0=pb[o2][H, :], in1=rb1[64:128, :], op=ALU.mult),
                                      reads=[f"pb{o2}", "rb1"], writes=[f"c2{ck}"])
                                fw.op(dve, lambda e: e.scalar_tensor_tensor(out=c1[H, :], in0=c2[H, :], scalar=lamc[H, l:l + 1], in1=c1[H, :],
                                                                            op0=ALU.mult, op1=ALU.add),
                                      reads=[f"c1{ck}", f"c2{ck}", "c_lamc"], writes=[f"c1{ck}"])
                                fw.op(act, lambda e: e.activation(out=csq[H, :], in_=c1[H, :], func=AF.Square), reads=[f"c1{ck}"], writes=["csq"])
                                fw.op(pe, lambda e: e.matmul(pb[7][H, :], lhsT=masks[H, 4, 0:64], rhs=csq[H, :], start=True, stop=True),
                                      reads=["csq", "c_masks"], writes=["pb7"])
                                fw.op(act, lambda e: e.activation(out=cl[H, :], in_=pb[7][H, :], func=AF.Ln, bias=cc[H, 0:1], scale=1.0 / 64),
                                      reads=["pb7", "c_cc"], writes=["cl"])
                                fw.op(act, lambda e: e.activation(out=cl[H, :], in_=cl[H, :], func=AF.Exp, scale=-0.5), reads=["cl"], writes=["cl"])
                                dst = oTs[h // 2][(h % 2) * 64:(h % 2) * 64 + 64, tsl(qt)]
                                fw.op(dve, lambda e: e.scalar_tensor_tensor(out=dst, in0=c1[H, :], scalar=1.0 - lam0, in1=cl[H, :],
                                                                            op0=ALU.mult, op1=ALU.mult),
                                      reads=[f"c1{ck}", "cl"], writes=[f"oT{h // 2}"])

                            for j in range(jmax + 1):
                                q0 = max(j, 4 * qt) * 128
                                n = (4 * qt + 4) * 128 - q0
                                c0 = q0 - qt * 512
                                diag = j >= 4 * qt
                                sbs = ((0, 1), (2, 7))[gi % 2]
                                pis = ((2 * gi) % 6, (2 * gi + 1) % 6)
                                gi += 1

                                def front(h=h, qa=qa, ka=ka, j=j, q0=q0, n=n, sbs=sbs, pis=pis, diag=diag):
                                    fw.op(pe, [lambda e, mp=mp: e.matmul(
                                        pb[sbs[mp]][:, 0:n], lhsT=ka[mp * 64:mp * 64 + 36, j * 128:(j + 1) * 128],
                                        rhs=qa[mp * 64:mp * 64 + 36, q0:q0 + n], start=True, stop=True) for mp in range(2)],
                                        reads=[f"qk{h}", f"qk{4 + h}"], writes=[f"pb{sbs[0]}", f"pb{sbs[1]}"])
                                    for mp in range(2):
                                        Pt = Pts[pis[mp]]
                                        fw.op(act, lambda e, Pt=Pt, mp=mp: e.activation(out=Pt[:, 0:n], in_=pb[sbs[mp]][:, 0:n], func=AF.Exp),
                                              reads=[f"pb{sbs[mp]}"], writes=[f"Pt{pis[mp]}"])
                                        if diag:
                                            fw.op(dve, lambda e, Pt=Pt: e.tensor_tensor(out=Pt[:, 0:128], in0=Pt[:, 0:128], in1=maskc[:, h, :], op=ALU.mult),
                                                  reads=[f"Pt{pis[mp]}", "c_maskc"], writes=[f"Pt{pis[mp]}"])

                                def back(h=h, j=j, n=n, c0=c0, obs=obs, pis=pis, jmax=jmax, fin=fin):
                                    for mp in range(2):
                                        Pt = Pts[pis[mp]]
                                        fw.op(pe, lambda e, Pt=Pt, mp=mp: e.matmul(
                                            pb[obs[mp]][:, c0:c0 + n], lhsT=Vt[:, j, h, :], rhs=Pt[:, 0:n], start=(j == 0), stop=(j == jmax),
                                            skip_group_check=True), reads=["Vt", f"Pt{pis[mp]}"], writes=[f"pb{obs[mp]}"])
                                    if j == jmax:
                                        fin()
                                steps.append((front, back))
                    run_pipe(steps, LA=2, mid=vproj)

                def attn_D(e32s, l32s, lbs, t1s):
                    st = []
                    tile_i = 0
                    for h in range(4):
                        for qt in range(4):
                            ob = 3 + tile_i % 2
                            tile_i += 1
                            jmax = 4 * qt + 3
                            for k, j in enumerate(range(jmax, -1, -1)):
                                q0 = max(j, 4 * qt) * 128
                                st.append(dict(h=h, qt=qt, ob=ob, k=k, K=jmax + 1, j=j, q0=q0, n=(4 * qt + 4) * 128 - q0,
                                               c0=q0 - qt * 512, diag=(j >= 4 * qt)))
                    G = len(st)
                    SBK = (0, 1, 2, 7)

                    def F1(g):
                        s = st[g]
                        h, j, q0, n = s["h"], s["j"], s["q0"], s["n"]
                        sb = SBK[g % 4]
                        e32, l32, lb = e32s[g % 2], l32s[g % 3], lbs[g % 3]
                        qa, ka = qk[h], qk[4 + h]
                        fw.op(pe, lambda e: e.matmul(
                            pb[sb][:, 0:n], lhsT=ka[:, j * 128:(j + 1) * 128], rhs=qa[:, q0:q0 + n],
                            start=True, stop=True), reads=[f"qk{h}", f"qk{4 + h}"], writes=[f"pb{sb}"])
                        fw.op(act, lambda e: e.activation(out=e32[:, 0:n], in_=pb[sb][:, 0:n], func=AF.Exp),
                              reads=[f"pb{sb}"], writes=[f"e32{g % 2}"])
                        fw.op(act, lambda e: e.activation(out=l32[:, 0:n], in_=e32[:, 0:n], func=AF.Ln, bias=cc[:, 1:2], scale=1.0),
                              reads=[f"e32{g % 2}", "c_cc"], writes=[f"l32{g % 3}"])

                    def F1b(g):
                        s = st[g]
                        n = s["n"]
                        l32, lb = l32s[g % 3], lbs[g % 3]
                        fw.op(dve, lambda e: e.tensor_copy(out=lb[:, 0:n], in_=l32[:, 0:n]),
                              reads=[f"l32{g % 3}"], writes=[f"lb{g % 3}"])
                        if s["diag"]:
                            fw.op(pool, lambda e: e.tensor_tensor(out=lb[:, 0:128], in0=lb[:, 0:128], in1=mD, op=ALU.mult),
                                  reads=[f"lb{g % 3}", "c_masks"], writes=[f"lb{g % 3}"])

                    def F3(g):
                        s = st[g]
                        n, c0, k = s["n"], s["c0"], s["k"]
                        sb = SBK[g % 4]
                        l32, t1, Pt = l32s[g % 3], t1s[g % 2], Pts[g % 3]
                        rbk = 5 + k % 2
                        fw.op(dve, lambda e: e.tensor_tensor(out=t1[:, 0:n], in0=pb[sb][:, 0:n], in1=l32[:, 0:n], op=ALU.subtract),
                              reads=[f"pb{sb}", f"l32{g % 3}"], writes=[f"t1{g % 2}"])
                        fw.op(dve, lambda e: e.tensor_tensor(out=t1[:, 0:n], in0=t1[:, 0:n], in1=pb[rbk][:, c0:c0 + n], op=ALU.subtract),
                              reads=[f"t1{g % 2}", f"pb{rbk}"], writes=[f"t1{g % 2}"])
                        fw.op(act, lambda e: e.activation(out=Pt[:, 0:n], in_=t1[:, 0:n], func=AF.Exp),
                              reads=[f"t1{g % 2}"], writes=[f"Pt{g % 3}"])
                        if s["diag"]:
                            fw.op(pool, lambda e: e.tensor_tensor(out=Pt[:, 0:128], in0=Pt[:, 0:128], in1=mD, op=ALU.mult),
                                  reads=[f"Pt{g % 3}", "c_masks"], writes=[f"Pt{g % 3}"])

                    def PE_U(g):
                        s = st[g]
                        n, c0, k = s["n"], s["c0"], s["k"]
                        lb = lbs[g % 3]
                        b, bo = 5 + k % 2, 5 + (k + 1) % 2
                        fw.op(pe, lambda e: e.matmul(pb[b][:, c0:c0 + n], lhsT=mU, rhs=lb[:, 0:n], start=(k == 0), stop=False,
                                                     skip_group_check=True),
                              reads=[f"lb{g % 3}", "c_masks"], writes=[f"pb{b}"])
                        if k == 0:
                            fw.op(pe, lambda e: e.matmul(pb[bo][:, c0:c0 + n], lhsT=ones_b, rhs=lb[:, 0:n], start=True, stop=False,
                                                         skip_group_check=True),
                                  reads=[f"lb{g % 3}", "c_masks"], writes=[f"pb{bo}"])

                    def PE_fix(g):
                        s = st[g]
                        k, K = s["k"], s["K"]
                        if k > K - 3:
                            return
                        s1 = st[g + 1]
                        b = 5 + k % 2
                        lb0, lb1 = lbs[g % 3], lbs[(g + 1) % 3]
                        fw.op(pe, [lambda e: e.matmul(pb[b][:, s["c0"]:s["c0"] + s["n"]], lhsT=mLI, rhs=lb0[:, 0:s["n"]], start=False, stop=False,
                                                      skip_group_check=True),
                                   lambda e: e.matmul(pb[b][:, s1["c0"]:s1["c0"] + s1["n"]], lhsT=ones_b, rhs=lb1[:, 0:s1["n"]], start=False, stop=False,
                                                      skip_group_check=True)],
                              reads=[f"lb{g % 3}", f"lb{(g + 1) % 3}", "c_masks"], writes=[f"pb{b}"])

                    def PE_PV(g):
                        s = st[g]
                        h, qt, ob, k, K, j, n, c0 = s["h"], s["qt"], s["ob"], s["k"], s["K"], s["j"], s["n"], s["c0"]
                        Pt = Pts[g % 3]
                        fw.op(pe, lambda e: e.matmul(
                            pb[ob][:, c0:c0 + n], lhsT=Vt[:, j, h, :], rhs=Pt[:, 0:n], start=(k == 0), stop=(k == K - 1),
                            skip_group_check=True), reads=["Vt", f"Pt{g % 3}"], writes=[f"pb{ob}"])
                        if k == K - 1:
                            dst = oTs[h // 2][(h % 2) * 64:(h % 2) * 64 + 64, tsl(qt)]
                            fw.op(act, lambda e: e.activation(out=dst, in_=pb[ob][0:64, :], func=AF.Copy),
                                  reads=[f"pb{ob}"], writes=[f"oT{h // 2}"])

                    for it in range(-2, G + 1):
                        if it == 0:
                            vproj()
                        if 0 <= it - 1 < G:
                            PE_fix(it - 1)
                        if 0 <= it + 2 < G:
                            F1(it + 2)
                        if 0 <= it < G:
                            F3(it)
                        if 0 <= it + 2 < G:
                            F1b(it + 2)
                        if 0 <= it + 1 < G:
                            PE_U(it + 1)
                        if 0 <= it - 1 < G:
                            PE_PV(it - 1)

                if m == 0:
                    def _attn():
                        with ExitStack() as _es:
                            FA = _es.enter_context(SB("FA", [128, 512], F32))
                            ACC = _es.enter_context(SB("ACC", [128, 512], F32))
                            RR = _es.enter_context(SB("RR", [128, 512], F32))
                            AQ = _es.enter_context(SB("AQ", [128, 512], F32))
                            HB = _es.enter_context(SB("HB", [128, 512], BF16))
                            MB = _es.enter_context(SB("MB", [128, 512], BF16))
                            LB = _es.enter_context(SB("LB", [128, 512], BF16))
                            carry = _es.enter_context(SB("carry", [128, 2], F32))
                            dtmps = [_es.enter_context(SB(f"dtmp{i}", [128, 128], F32)) for i in range(2)]
                            wfa = _es.enter_context(SB("wfa", [128, 8, 4, 6], BF16))
                            STs = [[_es.enter_context(SB(f"ST{w}{i}", [128, 512], BF16)) for i in range(2)] for w in range(2)]
                            fw.op(dve, lambda e: e.tensor_copy(
                                out=wfa[:], in_=bass.AP(wbuf, 768, [[8 * 772, 128], [772, 8], [1, 4], [0, 6]])),
                                reads=["wbuf"], writes=["wfa"])
                            R = slice(0, 24)
                            for tt in range(4):
                                bank = (5, 6, 7, 4)[tt]
                                fw.op(pe, [lambda e, kc=kc, tt=tt, bank=bank: e.matmul(
                                    pb[bank][R, :], lhsT=wfa[:, kc, :, :].rearrange("p h r -> p (h r)"), rhs=hT[:, kc, tsl(tt)],
                                    start=(kc == 0), stop=(kc == 7)) for kc in range(8)],
                                    reads=["wfa", f"h{tt}"], writes=[f"pb{bank}"])
                                fw.op(dve, lambda e, bank=bank: e.tensor_scalar(
                                    out=FA[R, :], in0=pb[bank][R, :], scalar1=bfb[R, l:l + 1], scalar2=None,
                                    op0=ALU.add), reads=[f"pb{bank}", "c_bfb"], writes=["FA"])
                                fw.op(act, lambda e: e.activation(out=FA[R, :], in_=FA[R, :], func=AF.Exp, scale=-1.0),
                                      reads=["FA"], writes=["FA"])
                                fw.op(act, lambda e: e.activation(out=FA[R, :], in_=FA[R, :], func=AF.Ln, bias=cc[R, 1:2], scale=1.0),
                                      reads=["FA", "c_cc"], writes=["FA"])
                                ini = 0.0 if tt == 0 else carry[R, 0:1]
                                fw.op(dve, lambda e, ini=ini: e.tensor_tensor_scan(out=ACC[R, :], data0=FA[R, :], data1=FA[R, :], initial=ini,
                                                                                   op0=ALU.add, op1=ALU.max),
                                      reads=["FA", "carry"], writes=["ACC"])
                                fw.op(dve, lambda e: e.tensor_copy(out=carry[R, 0:1], in_=ACC[R, 511:512]), reads=["ACC"], writes=["carry"])
                                fw.op(dve, lambda e: e.tensor_copy(out=HB[R, :], in_=ACC[R, :]), reads=["ACC"], writes=["HB"])
                                fw.op(dve, lambda e: e.tensor_tensor(out=RR[R, :], in0=ACC[R, :], in1=HB[R, :], op=ALU.subtract),
                                      reads=["ACC", "HB"], writes=["RR"])
                                fw.op(dve, lambda e: e.tensor_copy(out=MB[R, :], in_=RR[R, :]), reads=["RR"], writes=["MB"])
                                fw.op(dve, lambda e: e.tensor_tensor(out=ACC[R, :], in0=RR[R, :], in1=MB[R, :], op=ALU.subtract),
                                      reads=["RR", "MB"], writes=["ACC"])
                                fw.op(dve, lambda e: e.tensor_copy(out=LB[R, :], in_=ACC[R, :]), reads=["ACC"], writes=["LB"])
                                for w_, c0 in ((0, 2), (1, 6)):
                                    ST = STs[w_][tt % 2]
                                    stk = f"ST{w_}{tt % 2}"
                                    fw.op(dve, lambda e, c0=c0: e.tensor_scalar(
                                        out=AQ[R, :], in0=HB[R, :], scalar1=cc[R, c0:c0 + 1], scalar2=cc[R, c0 + 3:c0 + 4],
                                        op0=ALU.mult, op1=ALU.add), reads=["HB", "c_cc"], writes=["AQ"])
                                    fw.op(dve, lambda e, c0=c0: e.scalar_tensor_tensor(
                                        out=RR[R, :], in0=MB[R, :], scalar=cc[R, c0 + 1:c0 + 2], in1=AQ[R, :],
                                        op0=ALU.mult, op1=ALU.add), reads=["MB", "AQ", "c_cc"], writes=["RR"])
                                    fw.op(dve, lambda e, c0=c0, ST=ST: e.scalar_tensor_tensor(
                                        out=ST[R, :], in0=LB[R, :], scalar=cc[R, c0 + 2:c0 + 3], in1=RR[R, :],
                                        op0=ALU.mult, op1=ALU.add), reads=["LB", "RR", "c_cc"], writes=[stk])
                                    for h in range(4):
                                        idx = w_ * 4 + h
                                        fw.dma(sp, f"aug{idx}_{tt % 2}", qk[idx][64:70, tsl(tt)], ST[h * 6:(h + 1) * 6, :],
                                               reads=[stk], writes=[f"qa{idx}_{tt}"])
                            qkproj(act_only=True)
                            attn_A(dtmps)
                            fw.barrier()
                    _attn()
                if m == 1:
                    def _attn():
                        with ExitStack() as _es:
                            for idx_ in range(8):
                                fw.op(pool, lambda e, idx_=idx_: e.memset(qk[idx_][64:128, :], 0.0), writes=[f"qk{idx_}"])
                            qkproj()
                            EB = _es.enter_context(SB("EB", [128, 4, 640], F32))
                            visb = _es.enter_context(SB("sb_visb", [128, 640], BF16))
                            tbs = [_es.enter_context(SB(f"tb{i}", [128, 512], F32)) for i in range(2)]
                            fw.dma(sp, "eb", EB[:], relb_d[l], writes=["EB"])
                            fw.dma(sp, "eb2", visb[:], visb_d.ap(), writes=["visb"])
                            fw.op(act, lambda e: e.activation(out=EB[:], in_=EB[:], func=AF.Exp), reads=["EB"], writes=["EB"])
                            for h in range(4):
                                fw.op(dve, lambda e, h=h: e.tensor_tensor(out=EB[:, h, :], in0=EB[:, h, :], in1=visb[:], op=ALU.mult),
                                      reads=["EB", "visb"], writes=["EB"])
                            attn_B(EB, tbs)
                            fw.barrier()
                    _attn()
                if m == 2:
                    def _attn():
                        with ExitStack() as _es:
                            qkproj()
                            c1s = [_es.enter_context(SB(f"c1{i}", [128, 512], F32)) for i in range(2)]
                            c2s = [_es.enter_context(SB(f"c2{i}", [128, 512], F32)) for i in range(2)]
                            csq = _es.enter_context(SB("csq", [128, 512], BF16))
                            cl = _es.enter_context(SB("cl", [128, 512], F32))
                            for h in range(4):
                                for which in range(2):
                                    dst = qk[which * 4 + h]
                                    for mp in range(2):
                                        fw.dma(sp, f"cau{which * 4 + h}", dst[mp * 64 + 32:mp * 64 + 36, :], caug_d[h, which],
                                               writes=[f"qk{which * 4 + h}"])
                            attn_C(c1s, c2s, csq, cl)
                            fw.barrier()
                    _attn()
                if m == 3:
                    def _attn():
                        with ExitStack() as _es:
                            for idx_ in range(8):
                                fw.op(pool, lambda e, idx_=idx_: e.memset(qk[idx_][64:128, :], 0.0), writes=[f"qk{idx_}"])
                            qkproj()
                            e32s = [_es.enter_context(SB(f"e32{i}", [128, 512], F32)) for i in range(2)]
                            l32s = [_es.enter_context(SB(f"l32{i}", [128, 512], F32)) for i in range(3)]
                            lbs = [_es.enter_context(SB(f"lb{i}", [128, 512], BF16)) for i in range(3)]
                            t1s = [_es.enter_context(SB(f"t1{i}", [128, 512], F32)) for i in range(2)]
                            attn_D(e32s, l32s, lbs, t1s)
                            fw.barrier()
                    _attn()

                for oc in range(8):
                    for tt in range(4):
                        bank = 5 + ctr["w"] % 2
                        ctr["w"] += 1
                        fw.op(pe, [lambda e, pr=pr, oc=oc, tt=tt, bank=bank: e.matmul(
                            pb[bank][:], lhsT=wo[:, pr, oc * 128:(oc + 1) * 128], rhs=oTs[pr][:, tsl(tt)],
                            start=(pr == 0), stop=(pr == 1)) for pr in range(2)],
                            reads=["sq", "oT0", "oT1"], writes=[f"pb{bank}"])
                        fw.op(dve, lambda e, oc=oc, tt=tt, bank=bank: e.tensor_tensor(
                            out=xT[:, oc, tsl(tt)], in0=pb[bank][:], in1=xT[:, oc, tsl(tt)], op=ALU.add),
                            reads=[f"pb{bank}", xk(oc, tt)], writes=[xk(oc, tt)])
                fw.barrier()

    if stages is None:
        stages = []
        for l in range(NL):
            stages += [("ffn", l, 0), ("mix", l), ("ffn", l, 1)]
    for b in range(nseq):
        load_x(b)
        for stg in stages:
            if stg[0] == "ffn":
                ffn(stg[1], stg[2])
            else:
                mixer(stg[1])
        store_x(b, final_norm)
    nc._fw_stats = (fw.n_inst, fw.n_wait)
    return nc


def _host_inputs(inputs):
    f32 = np.float32
    g = np.stack([inputs["g_ffn1"][0], inputs["g_mix"][0], inputs["g_ffn2"][0],
                  inputs["g_ffn1"][1], inputs["g_mix"][1], inputs["g_ffn2"][1], inputs["g_final"]], 0).astype(f32)
    gall = np.ascontiguousarray(g.reshape(7, 8, 128).transpose(2, 0, 1).reshape(128, 56))
    bfb = np.zeros((128, 2), f32)
    bfb[0:24, :] = np.repeat(inputs["b_f"].astype(f32).T, 6, axis=0)
    dlb = np.ascontiguousarray(np.broadcast_to(inputs["diff_lambda"].astype(f32).reshape(1, 256), (128, 256)))
    kk = np.arange(128)[:, None]
    qq = np.arange(640)[None, :]
    ridx = np.clip(qq - kk, -256, 256) + 256
    relb = np.ascontiguousarray(inputs["rel_bias"].astype(f32)[:, :, ridx].transpose(0, 2, 1, 3))
    m = dict(_consts())
    m.update(gall=gall, bfb=bfb, dlb=dlb, relb=relb)
    for k in ("ffn1_w_gu", "ffn2_w_gu", "ffn1_w_down", "ffn2_w_down", "w_in", "w_out"):
        m[k] = np.ascontiguousarray(inputs[k], dtype=f32)
    return m


def kernel(**inputs):
    x = np.ascontiguousarray(inputs["x"], dtype=np.float32)
    shared = _host_inputs(inputs)
    nc = build()
    in_maps = []
    for c in range(NCORES):
        d = dict(shared)
        d["x"] = x[c * NSEQ:(c + 1) * NSEQ]
        in_maps.append(d)
    res = run_bass_kernel_spmd(nc, in_maps, core_ids=list(range(NCORES)))
    return np.concatenate([np.asarray(r["y"], dtype=np.float32) for r in res.results], axis=0)
```
